# Optimizing a Trainium2 kernel written in Bass

```python
import math
import jax, jax.numpy as jnp
from jax import lax
import numpy as np

D_MODEL = 2048
BATCH = 4
SEQ = 2048
DEPTH = 2
DEC_BATCH = 8
DEC_SEQ = 4096
PAST_LEN = 128

PLE_DIM = 256
EPS = 1e-6
CONV_CH = 512
CONV_WIDTH = 31
CONV_PAD = (CONV_WIDTH - 1) // 2
SSD_HEAD_DIM = 64
SSD_HEADS = 12
SSD_WIDTH = SSD_HEADS * SSD_HEAD_DIM
SSD_GROUPS = 2
SSD_HPG = SSD_HEADS // SSD_GROUPS
SSD_STATE = 128
SSD_CONV_WIDTH = 5
SSD_CONV_PAD = (SSD_CONV_WIDTH - 1) // 2
SSD_CHUNK = 128
SSD_XBC = SSD_WIDTH + 2 * SSD_GROUPS * SSD_STATE
ATT_HEADS = 6
ATT_HEAD_DIM = 64
ATT_WIDTH = ATT_HEADS * 2 * ATT_HEAD_DIM
Q_BLOCK = 128
REL_BUCKETS = 32
REL_MAX_DIST = 128
MIX_WIDTH = CONV_CH + SSD_WIDTH + ATT_WIDTH
D_FF = 4 * D_MODEL
IN_SIZES = (CONV_CH, CONV_CH, SSD_WIDTH, SSD_XBC, 2 * SSD_HEADS, ATT_WIDTH, ATT_WIDTH, ATT_WIDTH)
IN_COLS = sum(IN_SIZES)
IN_SPLITS = tuple(int(v) for v in np.cumsum(IN_SIZES)[:-1])

kernel_name = "hybrid_bidir_conv_ssd_diffattn_encoder"


def rmsnorm(x, g):
    xf = x.astype(jnp.float32)
    y = xf * lax.rsqrt(jnp.mean(xf * xf, axis=-1, keepdims=True) + EPS)
    return (y * g.astype(jnp.float32)).astype(x.dtype)


def layernorm(x, g, b):
    xf = x.astype(jnp.float32)
    mu = jnp.mean(xf, axis=-1, keepdims=True)
    var = jnp.mean(jnp.square(xf - mu), axis=-1, keepdims=True)
    y = (xf - mu) * lax.rsqrt(var + EPS)
    return (y * g.astype(jnp.float32) + b.astype(jnp.float32)).astype(x.dtype)


def depthwise_conv(x, w, pad):
    c = x.shape[-1]
    return lax.conv_general_dilated(
        x, w[:, None, :].astype(x.dtype), window_strides=(1,), padding=[(pad, pad)],
        dimension_numbers=("NWC", "WIO", "NWC"), feature_group_count=c)


def rel_bucket(rel):
    nb = REL_BUCKETS // 2
    max_exact = nb // 2
    ret = jnp.where(rel > 0, nb, 0)
    n = jnp.abs(rel)
    nf = jnp.maximum(n, 1).astype(jnp.float32)
    large = max_exact + (jnp.log(nf / max_exact) / math.log(REL_MAX_DIST / max_exact)
                         * (nb - max_exact)).astype(jnp.int32)
    large = jnp.minimum(large, nb - 1)
    return ret + jnp.where(n < max_exact, n, large)


def ssd_scan(xh, dt, A, B, C):
    b, l, G, R, P = xh.shape
    N = B.shape[-1]
    c = l // SSD_CHUNK
    Q = SSD_CHUNK
    dt_ = xh.dtype
    xh = xh.reshape(b, c, Q, G, R, P)
    dt = dt.reshape(b, c, Q, G, R)
    B = B.reshape(b, c, Q, G, N)
    C = C.reshape(b, c, Q, G, N)
    a_cs = jnp.cumsum((dt * A).astype(jnp.float32), axis=2)
    xdt = xh * dt[..., None]
    a_t = jnp.moveaxis(a_cs, 2, -1)
    seg = a_t[..., :, None] - a_t[..., None, :]
    tril = jnp.tril(jnp.ones((Q, Q), dtype=bool))
    L = jnp.exp(jnp.where(tril, seg, -jnp.inf)).astype(dt_)
    CB = jnp.einsum("bclgn,bcsgn->bcgls", C, B)
    y_diag = jnp.einsum("bcgls,bcgrls,bcsgrp->bclgrp", CB, L, xdt)
    decay_states = jnp.exp(a_cs[:, :, -1:] - a_cs).astype(dt_)
    states = jnp.einsum("bcsgn,bcsgr,bcsgrp->bcgrpn", B, decay_states, xdt)
    chunk_decay = jnp.exp(a_cs[:, :, -1]).astype(dt_)

    def step(carry, inp):
        st, dec = inp
        return carry * dec[..., None, None] + st, carry

    _, prev = lax.scan(step, jnp.zeros_like(states[:, 0]),
                       (jnp.moveaxis(states, 1, 0), jnp.moveaxis(chunk_decay, 1, 0)))
    prev = jnp.moveaxis(prev, 0, 1)
    y_off = jnp.einsum("bclgn,bcgrpn,bclgr->bclgrp", C, prev, jnp.exp(a_cs).astype(dt_))
    return (y_diag + y_off).reshape(b, l, G, R, P)


def conv_module(val, gate, conv_w, conv_b, norm_g, norm_b):
    u = val * jax.nn.sigmoid(gate)
    u = depthwise_conv(u, conv_w, CONV_PAD) + conv_b
    u = layernorm(u, norm_g, norm_b)
    return jax.nn.silu(u)


def ssd_mixer(z, xbc, dt_raw, conv_w, conv_b, dt_bias, a_log, d_skip, norm_g):
    b, l, _ = xbc.shape
    xbc = jax.nn.silu(depthwise_conv(xbc, conv_w, SSD_CONV_PAD) + conv_b)
    xs, Bm, Cm = jnp.split(xbc, [SSD_WIDTH, SSD_WIDTH + SSD_GROUPS * SSD_STATE], axis=-1)
    xh = xs.reshape(b, l, SSD_GROUPS, SSD_HPG, SSD_HEAD_DIM)
    Bm = Bm.reshape(b, l, SSD_GROUPS, SSD_STATE)
    Cm = Cm.reshape(b, l, SSD_GROUPS, SSD_STATE)
    dt = jax.nn.softplus(dt_raw.reshape(b, l, 2, SSD_GROUPS, SSD_HPG)
                         + dt_bias.reshape(2, SSD_GROUPS, SSD_HPG))
    A = -jnp.exp(a_log.reshape(2, SSD_GROUPS, SSD_HPG))
    y_f = ssd_scan(xh, dt[:, :, 0], A[0], Bm, Cm)
    fl = lambda t: jnp.flip(t, axis=1)
    y_b = fl(ssd_scan(fl(xh), fl(dt[:, :, 1]), A[1], fl(Bm), fl(Cm)))
    y = y_f + y_b + d_skip.reshape(SSD_GROUPS, SSD_HPG)[:, :, None] * xh
    y = y.reshape(b, l, SSD_WIDTH)
    return rmsnorm(y * jax.nn.silu(z), norm_g)


def diff_attention(q, k, v, lq1, lk1, lq2, lk2, subln_g, rel_bias, lam_init):
    b, s, _ = q.shape
    q = q.reshape(b, s, ATT_HEADS, 2, ATT_HEAD_DIM) * (ATT_HEAD_DIM ** -0.5)
    k = k.reshape(b, s, ATT_HEADS, 2, ATT_HEAD_DIM)
    v = v.reshape(b, s, ATT_HEADS, 2 * ATT_HEAD_DIM)
    f32 = jnp.float32
    lam = (jnp.exp(jnp.sum(lq1.astype(f32) * lk1.astype(f32)))
           - jnp.exp(jnp.sum(lq2.astype(f32) * lk2.astype(f32))) + lam_init)
    n_blk = s // Q_BLOCK
    q_blocks = jnp.moveaxis(q.reshape(b, n_blk, Q_BLOCK, ATT_HEADS, 2, ATT_HEAD_DIM), 1, 0)
    k_pos = jnp.arange(s, dtype=jnp.int32)

    def block(args):
        q_blk, start = args
        logits = jnp.einsum("bqhmd,bkhmd->bhmqk", q_blk, k).astype(f32)
        rel = k_pos[None, :] - (start + jnp.arange(Q_BLOCK, dtype=jnp.int32))[:, None]
        bias = jnp.transpose(rel_bias[rel_bucket(rel)], (2, 0, 1)).astype(f32)
        probs = jax.nn.softmax(logits + bias[None, :, None], axis=-1)
        a = (probs[:, :, 0] - lam * probs[:, :, 1]).astype(v.dtype)
        return jnp.einsum("bhqk,bkhe->bqhe", a, v)

    o = lax.map(block, (q_blocks, jnp.arange(n_blk, dtype=jnp.int32) * Q_BLOCK))
    o = jnp.moveaxis(o, 0, 1).reshape(b, s, ATT_HEADS, 2 * ATT_HEAD_DIM)
    o = rmsnorm(o, subln_g) * (1.0 - lam_init)
    return o.reshape(b, s, ATT_WIDTH)


def setup_inputs(seed: int = 0) -> dict:
    key = jax.random.key(seed)
    ks = jax.random.split(key, 32)
    nrm = lambda k, shape, scale: jax.random.normal(k, shape, jnp.float32) * scale
    gain = lambda k, shape: 1.0 + 0.02 * jax.random.normal(k, shape, jnp.float32)
    dt = jnp.exp(jax.random.uniform(ks[10], (DEPTH, 2, SSD_HEADS), jnp.float32,
                                    math.log(1e-3), math.log(1e-1)))
    dt_bias = dt + jnp.log(-jnp.expm1(-dt))
    a_log = jnp.log(jax.random.uniform(ks[11], (DEPTH, 2, SSD_HEADS), jnp.float32, 1.0, 16.0))
    return {
        "x_prompt": nrm(ks[0], (BATCH, SEQ, D_MODEL), 1.0),
        "x_sample": nrm(ks[1], (DEC_BATCH, DEC_SEQ, D_MODEL), 1.0),
        "p_prompt": nrm(ks[2], (DEPTH, BATCH, SEQ, PLE_DIM), 1.0),
        "p_sample": nrm(ks[3], (DEPTH, DEC_BATCH, DEC_SEQ, PLE_DIM), 1.0),
        "norm_mix_g": gain(ks[4], (DEPTH, D_MODEL)),
        "w_in": nrm(ks[5], (DEPTH, D_MODEL, IN_COLS), D_MODEL ** -0.5),
        "conv_w": nrm(ks[6], (DEPTH, CONV_WIDTH, CONV_CH), CONV_WIDTH ** -0.5),
        "conv_b": nrm(ks[7], (DEPTH, CONV_CH), 0.02),
        "conv_norm_g": gain(ks[8], (DEPTH, CONV_CH)),
        "conv_norm_b": nrm(ks[9], (DEPTH, CONV_CH), 0.02),
        "ssd_conv_w": nrm(ks[12], (DEPTH, SSD_CONV_WIDTH, SSD_XBC), SSD_CONV_WIDTH ** -0.5),
        "ssd_conv_b": nrm(ks[13], (DEPTH, SSD_XBC), 0.02),
        "ssd_dt_bias": dt_bias,
        "ssd_a_log": a_log,
        "ssd_d": gain(ks[14], (DEPTH, SSD_HEADS)),
        "ssd_norm_g": gain(ks[15], (DEPTH, SSD_WIDTH)),
        "lambda_q1": nrm(ks[16], (DEPTH, ATT_HEAD_DIM), 0.1),
        "lambda_k1": nrm(ks[17], (DEPTH, ATT_HEAD_DIM), 0.1),
        "lambda_q2": nrm(ks[18], (DEPTH, ATT_HEAD_DIM), 0.1),
        "lambda_k2": nrm(ks[19], (DEPTH, ATT_HEAD_DIM), 0.1),
        "attn_subln_g": gain(ks[20], (DEPTH, 2 * ATT_HEAD_DIM)),
        "rel_bias": nrm(ks[21], (REL_BUCKETS, ATT_HEADS), 0.5),
        "w_out": nrm(ks[22], (DEPTH, MIX_WIDTH, D_MODEL), MIX_WIDTH ** -0.5),
        "norm_mlp_g": gain(ks[23], (DEPTH, D_MODEL)),
        "w_up": nrm(ks[24], (DEPTH, D_MODEL, D_FF), D_MODEL ** -0.5),
        "w_down": nrm(ks[25], (DEPTH, D_FF, D_MODEL), D_FF ** -0.5),
        "norm_ple_g": gain(ks[26], (DEPTH, D_MODEL)),
        "w_ple": nrm(ks[27], (DEPTH, PLE_DIM, D_MODEL), PLE_DIM ** -0.5),
        "w_ple_gate": nrm(ks[28], (DEPTH, D_MODEL, D_MODEL), D_MODEL ** -0.5),
        "final_norm_g": gain(ks[29], (D_MODEL,)),
    }


def reference(x_prompt, x_sample, p_prompt, p_sample, norm_mix_g, w_in, conv_w, conv_b,
              conv_norm_g, conv_norm_b, ssd_conv_w, ssd_conv_b, ssd_dt_bias, ssd_a_log, ssd_d,
              ssd_norm_g, lambda_q1, lambda_k1, lambda_q2, lambda_k2, attn_subln_g, rel_bias,
              w_out, norm_mlp_g, w_up, w_down, norm_ple_g, w_ple, w_ple_gate, final_norm_g):

    def run(x, p):
        h = x
        for i in range(DEPTH):
            lam_init = 0.8 - 0.6 * math.exp(-0.3 * i)
            u = rmsnorm(h, norm_mix_g[i])
            proj = jnp.einsum("bsd,dc->bsc", u, w_in[i])
            c_val, c_gate, s_z, s_xbc, s_dt, a_q, a_k, a_v = jnp.split(proj, IN_SPLITS, axis=-1)
            y_conv = conv_module(c_val, c_gate, conv_w[i], conv_b[i], conv_norm_g[i], conv_norm_b[i])
            y_ssd = ssd_mixer(s_z, s_xbc, s_dt, ssd_conv_w[i], ssd_conv_b[i], ssd_dt_bias[i],
                              ssd_a_log[i], ssd_d[i], ssd_norm_g[i])
            y_att = diff_attention(a_q, a_k, a_v, lambda_q1[i], lambda_k1[i], lambda_q2[i],
                                   lambda_k2[i], attn_subln_g[i], rel_bias, lam_init)
            mix = jnp.concatenate([y_conv, y_ssd, y_att], axis=-1)
            h = h + jnp.einsum("bsc,cd->bsd", mix, w_out[i])
            u = rmsnorm(h, norm_mlp_g[i])
            hid = jnp.square(jax.nn.relu(jnp.einsum("bsd,df->bsf", u, w_up[i])))
            h = h + jnp.einsum("bsf,fd->bsd", hid, w_down[i])
            gate = jax.nn.sigmoid(jnp.einsum("bsd,de->bse", rmsnorm(h, norm_ple_g[i]), w_ple_gate[i]))
            h = h + jnp.einsum("bsk,kd->bsd", p[i], w_ple[i]) * gate
        return rmsnorm(h, final_norm_g)

    y_prompt = run(x_prompt, p_prompt)
    y_sample = run(x_sample, p_sample)
    return (y_prompt, y_sample)
```

```python
import math
from contextlib import ExitStack

import numpy as np
import concourse.bass as bass
import concourse.mybir as mybir
from concourse.bass_utils import run_bass_kernel_spmd

F32 = mybir.dt.float32
BF16 = mybir.dt.bfloat16
ALU = mybir.AluOpType
AF = mybir.ActivationFunctionType

D = 2048
NDC = 16
DEPTH = 2
PLE = 256
EPS = 1e-6
CONV_CH = 512
CONV_W = 31
SSD_W = 768
SSD_H = 12
SSD_XBC = 1280
SSD_CW = 5
ATT_H = 6
D_FF = 8192
IN_COLS = 5400
T = 512
NBIAS = 1279

ENGS = ("sync", "scalar", "vector", "gpsimd", "tensor")


class Ev:
    __slots__ = ("sem", "val")

    def __init__(self, sem, val=None):
        self.sem = sem
        self.val = val


class Buf:
    __slots__ = ("name", "writers", "readers", "sem", "semcnt", "grp", "glob", "gen")

    def __init__(self, name, glob=False):
        self.name = name
        self.writers = []
        self.readers = []
        self.sem = None
        self.semcnt = 0
        self.grp = None
        self.glob = glob
        self.gen = []


class Rec:
    def __init__(self, nc, es):
        self.nc = nc
        self.es = es
        self.q = {e: [] for e in ENGS}
        self.cnt = {e: 0 for e in ENGS}
        self.esem = {e: es.enter_context(nc.semaphore("s_" + e)) for e in ENGS}
        self.pending = {e: [] for e in ENGS}
        self.pool = []
        self.local = []
        self.nsem = len(ENGS)
        self.bar = {e: [] for e in ENGS}

    def _deps(self, reads, writes, adds):
        deps = []
        for b in reads:
            deps += b.writers
            b.grp = None
        for b in writes:
            deps += b.writers
            deps += b.readers
            b.grp = None
        for b in adds:
            if b.readers:
                b.gen = list(b.readers) + list(b.writers)
            deps += b.gen
        return deps

    def _commit(self, ev, reads, writes, adds):
        for b in writes:
            b.writers = [ev]
            b.readers = []
            b.gen = []
        for b in adds:
            if b.readers:
                b.writers = [ev]
                b.readers = []
            else:
                if not b.writers or b.writers[-1] is not ev:
                    b.writers.append(ev)
                if len(b.writers) > 16:
                    b.writers = self._compact(b.writers)
        for b in reads:
            if not b.readers or b.readers[-1] is not ev:
                b.readers.append(ev)
            if len(b.readers) > 16:
                b.readers = self._compact(b.readers)

    @staticmethod
    def _compact(evs):
        best = {}
        keep = []
        for e in evs:
            if e.val is None:
                keep.append(e)
                continue
            k = id(e.sem)
            if k not in best or e.val > best[k].val:
                best[k] = e
        return keep + list(best.values())

    def op(self, eng, fn, reads=(), writes=(), adds=(), signal=True):
        deps = self._deps(reads, writes, adds) + self.bar[eng]
        self.bar[eng] = []
        ev = Ev(self.esem[eng])
        if signal:
            self.cnt[eng] += 1
            ev.val = self.cnt[eng]
            for p in self.pending[eng]:
                p.val = ev.val
            self.pending[eng] = []
        else:
            self.pending[eng].append(ev)
        self.q[eng].append((deps, fn, (self.esem[eng], 1) if signal else None, eng == "tensor"))
        self._commit(ev, reads, writes, adds)
        return ev

    def dma(self, eng, fn, home, reads=(), writes=(), adds=()):
        deps = self._deps(reads, writes, adds) + self.bar[eng]
        self.bar[eng] = []
        if home.sem is None:
            if self.pool and not home.glob:
                home.sem, home.semcnt = self.pool.pop()
            else:
                home.sem = self.es.enter_context(self.nc.semaphore("d%d" % self.nsem))
                home.semcnt = 0
                self.nsem += 1
            if not home.glob:
                self.local.append(home)
        home.semcnt += 16
        if home.grp is None:
            home.grp = Ev(home.sem)
        ev = home.grp
        ev.val = home.semcnt
        self.q[eng].append((deps, fn, (home.sem, 16), False))
        self._commit(ev, reads, writes, adds)
        home.grp = ev
        return ev

    def barrier(self):
        for e in ENGS:
            assert not self.pending[e], "unsignaled tail on " + e
        evs = [Ev(self.esem[e], self.cnt[e]) for e in ENGS if self.cnt[e] > 0]
        for b in self.local:
            evs.append(Ev(b.sem, b.semcnt))
            self.pool.append((b.sem, b.semcnt))
            b.sem = None
        self.local = []
        for e in ENGS:
            self.bar[e] = self.bar[e] + evs

    def replay(self, final_waits=()):
        nc = self.nc
        for e in ENGS:
            assert not self.pending[e], "unsignaled tail on " + e
        with nc.Block() as block:
            def run(engname, engobj):
                waited = {}
                for deps, fn, inc, is_pe in self.q[engname]:
                    need = {}
                    for d in deps:
                        if is_pe and d.sem is self.esem["tensor"]:
                            continue
                        k = id(d.sem)
                        if d.val > waited.get(k, 0) and d.val > need.get(k, (None, 0))[1]:
                            need[k] = (d.sem, d.val)
                    for k, (sem, val) in need.items():
                        engobj.wait_ge(sem, val)
                        waited[k] = val
                    ins = fn(engobj)
                    if inc is not None:
                        ins.then_inc(inc[0], inc[1])
                if engname == "sync":
                    for sem, val in final_waits:
                        engobj.wait_ge(sem, val)

            @block.sync
            def _(e):
                run("sync", e)

            @block.scalar
            def _(e):
                run("scalar", e)

            @block.vector
            def _(e):
                run("vector", e)

            @block.gpsimd
            def _(e):
                run("gpsimd", e)

            @block.tensor
            def _(e):
                run("tensor", e)


class Tl:
    __slots__ = ("ap", "buf")

    def __init__(self, ap, buf):
        self.ap = ap
        self.buf = buf

    def __getitem__(self, k):
        return self.ap[k]


def rel_bucket_np(rel):
    nb = 16
    max_exact = 8
    ret = np.where(rel > 0, nb, 0)
    n = np.abs(rel)
    nf = np.maximum(n, 1).astype(np.float32)
    large = max_exact + (np.log(nf / np.float32(max_exact)) / np.float32(math.log(128 / max_exact))
                         * np.float32(nb - max_exact)).astype(np.int32)
    large = np.minimum(large, nb - 1)
    return ret + np.where(n < max_exact, n, large)


def onehot_bias_table():
    s = np.arange(NBIAS)
    b = rel_bucket_np(639 - s)
    oh = np.zeros((32, NBIAS), np.float32)
    oh[b, s] = 1.0
    return oh


W_IN_BLOCKS = [(0, 512), (512, 512), (1024, 512), (1536, 256), (1792, 512), (2304, 512), (2816, 256),
               (3072, 24), (3096, 512), (3608, 256), (3864, 512), (4376, 256), (4632, 512), (5144, 256)]


def build(SA, SB, debug=False, stop_after=None, nlayers=DEPTH):
    nc = bass.Bass("TRN2", target_bir_lowering=False)
    STOT = SA + SB
    seqs = [("a", SA, 0), ("b", SB, SA)]

    def din(name, shape, dt=F32):
        return nc.dram_tensor(name, list(shape), dt, kind="ExternalInput").ap()

    def dint(name, shape, dt=F32):
        kind = "ExternalOutput" if (debug and name != "wsc") else "Internal"
        return nc.dram_tensor(name, list(shape), dt, kind=kind).ap()

    def dout(name, shape, dt=F32):
        return nc.dram_tensor(name, list(shape), dt, kind="ExternalOutput").ap()

    x_in = {"a": din("xa", [SA, D]), "b": din("xb", [SB, D])}
    p_in = {"a": din("pa", [DEPTH, SA, PLE]), "b": din("pb", [DEPTH, SB, PLE])}
    y_out = {"a": dout("ya", [SA, D]), "b": dout("yb", [SB, D])}
    w_in = din("w_in", [DEPTH, D, IN_COLS])
    w_out = din("w_out", [DEPTH, D, D])
    w_up = din("w_up", [DEPTH, D, D_FF])
    w_down = din("w_down", [DEPTH, D_FF, D])
    w_ple = din("w_ple", [DEPTH, PLE, D])
    w_gate = din("w_ple_gate", [DEPTH, D, D])
    norm_mix_g = din("norm_mix_g", [DEPTH, D])
    norm_mlp_g = din("norm_mlp_g", [DEPTH, D])
    norm_ple_g = din("norm_ple_g", [DEPTH, D])
    final_norm_g = din("final_norm_g", [D])
    conv_w = din("conv_w", [DEPTH, CONV_W, CONV_CH])
    conv_b = din("conv_b", [DEPTH, CONV_CH])
    conv_norm_g = din("conv_norm_g", [DEPTH, CONV_CH])
    conv_norm_b = din("conv_norm_b", [DEPTH, CONV_CH])
    ssd_conv_w = din("ssd_conv_w", [DEPTH, SSD_CW, SSD_XBC])
    ssd_conv_b = din("ssd_conv_b", [DEPTH, SSD_XBC])
    ssd_dt_bias = din("ssd_dt_bias", [DEPTH, 24])
    ssd_a_log = din("ssd_a_log", [DEPTH, 24])
    ssd_d = din("ssd_d", [DEPTH, SSD_H])
    ssd_norm_g = din("ssd_norm_g", [DEPTH, SSD_W])
    lam_in = [din(n, [DEPTH, 64]) for n in ("lambda_q1", "lambda_k1", "lambda_q2", "lambda_k2")]
    subln_g = din("attn_subln_g", [DEPTH, 128])
    rel_bias = din("rel_bias", [32, ATT_H])
    onehot = din("onehot", [32, NBIAS])

    hTd = dint("hTd", [NDC, 128, STOT])
    gluT = dint("gluT", [4, 128, STOT], BF16)
    xbcT = dint("xbcT", [10, 128, STOT], BF16)
    qTd = dint("qTd", [6, 128, STOT], BF16)
    kTd = dint("kTd", [6, 128, STOT], BF16)
    vd = dint("vd", [STOT, 768], BF16)
    zsd = dint("zsd", [STOT, 768], BF16)
    dtd = dint("dtd", [STOT, 24])
    yfd = dint("yfd", [STOT, 768])
    mixT = dint("mixT", [NDC, 128, STOT], BF16)
    wvd = dint("wvd", [ATT_H, NBIAS])

    wspec = {}
    woff = 0
    for l in range(DEPTH):
        for name, src, K, blocks in (
            ("in", w_in, D, W_IN_BLOCKS),
            ("out", w_out, D, [(i * 512, 512) for i in range(4)]),
            ("up", w_up, D, [(i * 512, 512) for i in range(16)]),
            ("down", w_down, D_FF, [(i * 128, 128) for i in range(16)]),
            ("gate", w_gate, D, [(i * 512, 512) for i in range(4)]),
            ("ple", w_ple, PLE, [(0, 2048)]),
        ):
            kc = K // 128
            lst = []
            for (c0, bw) in blocks:
                lst.append((woff, kc, c0, bw))
                woff += 128 * kc * bw
            wspec[(l, name)] = (src, lst)
    wsc = dint("wsc", [woff], BF16)

    def wdst(off, kc, bw):
        return bass.AP(wsc.tensor, off, [[kc * bw, 128], [bw, kc], [1, bw]])

    es = ExitStack()
    with es:
        R = Rec(nc, es)
        ARENA = 53200
        arena = es.enter_context(nc.sbuf_tensor("arena", [128, ARENA], F32))
        psum = es.enter_context(nc.psum_tensor("psum", [128, 8, 512], F32))
        st = {"off": 0, "gbase": 0, "n": 0, "top": ARENA}

        def alloc(shape, dt=F32, name=None, buf=None, glob=False):
            n = int(np.prod(shape))
            n32 = n if dt == F32 else (n + 1) // 2
            assert st["off"] + n32 <= ARENA, "SBUF arena overflow %d" % (st["off"] + n32)
            a = arena[:, st["off"]:st["off"] + n32]
            st["off"] += n32
            if dt != F32:
                a = a.bitcast(BF16)[:, 0:n]
            if len(shape) == 2:
                a = a.rearrange("p (a b) -> p a b", a=shape[0])
            elif len(shape) == 3:
                a = a.rearrange("p (a b c) -> p a b c", a=shape[0], b=shape[1])
            st["n"] += 1
            return Tl(a, buf if buf is not None else Buf(name or ("t%d" % st["n"]), glob=glob))

        def talloc(shape, dt=F32, name=None):
            n = int(np.prod(shape))
            n32 = n if dt == F32 else (n + 1) // 2
            st["top"] -= n32
            assert st["top"] >= st["off"], "arena overflow (temp)"
            a = arena[:, st["top"]:st["top"] + n32]
            if dt != F32:
                a = a.bitcast(BF16)[:, 0:n]
            if len(shape) == 2:
                a = a.rearrange("p (a b) -> p a b", a=shape[0])
            st["n"] += 1
            return Tl(a, Buf(name or ("tt%d" % st["n"])))

        def reset_arena():
            st["off"] = st["gbase"]

        def pbank(i, buf=None):
            return Tl(psum[:, i, :], buf if buf is not None else Buf("ps%d" % i))

        def bufs_of(lst):
            return [t.buf if isinstance(t, Tl) else t for t in lst]

        def V(fn, reads=(), writes=(), adds=(), eng="vector"):
            return R.op(eng, fn, bufs_of(reads), bufs_of(writes), bufs_of(adds))

        def A(fn, reads=(), writes=(), adds=()):
            return R.op("scalar", fn, bufs_of(reads), bufs_of(writes), bufs_of(adds))

        def G(fn, reads=(), writes=(), adds=()):
            return R.op("gpsimd", fn, bufs_of(reads), bufs_of(writes), bufs_of(adds))

        def MM(out, lhsT, rhs, start, stop, reads, acc, signal, **kw):
            return R.op("tensor",
                        lambda e: e.matmul(out, lhsT=lhsT, rhs=rhs, start=start, stop=stop, **kw),
                        bufs_of(reads), (), bufs_of([acc]), signal=signal)

        def TR(out, in_, ident, reads, acc, signal):
            return R.op("tensor", lambda e: e.transpose(out=out, in_=in_, identity=ident),
                        bufs_of(reads), (), bufs_of([acc]), signal=signal)

        def LD(out_tl, out_ap, in_ap, reads=(), mode="w", q="sync"):
            if mode == "w":
                return R.dma(q, lambda e: e.dma_start(out=out_ap, in_=in_ap), out_tl.buf,
                             bufs_of(reads), [out_tl.buf], ())
            return R.dma(q, lambda e: e.dma_start(out=out_ap, in_=in_ap), out_tl.buf,
                         bufs_of(reads), (), [out_tl.buf])

        def ST(src_tl, out_ap, in_ap, dbuf, q="sync", mode="a"):
            if mode == "a":
                return R.dma(q, lambda e: e.dma_start(out=out_ap, in_=in_ap), src_tl.buf,
                             [src_tl.buf], (), [dbuf])
            return R.dma(q, lambda e: e.dma_start(out=out_ap, in_=in_ap), src_tl.buf,
                         [src_tl.buf], [dbuf], ())

        dbufs = {}

        def DB(*key):
            if key not in dbufs:
                dbufs[key] = Buf("dr_" + "_".join(str(k) for k in key), glob=True)
            return dbufs[key]

        out_bufs = []

        ident_f = alloc([128], F32, "ident_f")
        ident_b = alloc([128], BF16, "ident_b")
        J_b = alloc([128], BF16, "J_b")
        U_f = alloc([128], F32, "U_f")
        L_f = alloc([128], F32, "L_f")
        ones_f = alloc([128], F32, "ones_f")
        ones_b = alloc([128], BF16, "ones_b")
        ones512_b = alloc([128], BF16, "ones512")
        tmpc = talloc([128], F32, "tmpc")

        G(lambda e: e.memset(ident_f.ap, 0.0), writes=[ident_f])
        G(lambda e: e.affine_select(out=ident_f.ap, in_=ident_f.ap, pattern=[[-1, 128]], compare_op=ALU.not_equal,
                                    fill=1.0, base=0, channel_multiplier=1), reads=[ident_f], writes=[ident_f])
        G(lambda e: e.memset(ones_f.ap, 1.0), writes=[ones_f])
        G(lambda e: e.memset(tmpc.ap, 0.0), writes=[tmpc])
        G(lambda e: e.affine_select(out=tmpc.ap, in_=tmpc.ap, pattern=[[1, 128]], compare_op=ALU.not_equal,
                                    fill=1.0, base=-127, channel_multiplier=1), reads=[tmpc], writes=[tmpc])
        G(lambda e: e.affine_select(out=U_f.ap, in_=ones_f.ap, pattern=[[1, 128]], compare_op=ALU.is_ge,
                                    fill=0.0, base=0, channel_multiplier=-1), reads=[ones_f], writes=[U_f])
        G(lambda e: e.affine_select(out=L_f.ap, in_=ones_f.ap, pattern=[[-1, 128]], compare_op=ALU.is_ge,
                                    fill=0.0, base=0, channel_multiplier=1), reads=[ones_f], writes=[L_f])
        V(lambda e: e.tensor_copy(out=ident_b.ap, in_=ident_f.ap), reads=[ident_f], writes=[ident_b])
        V(lambda e: e.tensor_copy(out=J_b.ap, in_=tmpc.ap), reads=[tmpc], writes=[J_b])
        V(lambda e: e.tensor_copy(out=ones_b.ap, in_=ones_f.ap), reads=[ones_f], writes=[ones_b])
        V(lambda e: e.tensor_scalar_mul(out=ones512_b.ap, in0=ones_f.ap, scalar1=1.0 / 512.0), reads=[ones_f],
          writes=[ones512_b])

        wbuf = {}
        for l in range(DEPTH):
            for name in ("in", "out", "up", "down", "gate", "ple"):
                src, lst = wspec[(l, name)]
                b = Buf("w_%d_%s" % (l, name), glob=True)
                for bi, (off, kc, c0, bw) in enumerate(lst):
                    if l == 0 and name == "in":
                        b = Buf("w_%d_%s_%d" % (l, name, bi), glob=True)
                    wbuf[(l, name, bi)] = b

        def convert_weights(l):
            for name in ("in", "out", "up", "down", "gate", "ple"):
                src, lst = wspec[(l, name)]
                for bi, (off, kc, c0, bw) in enumerate(lst):
                    b = wbuf[(l, name, bi)]
                    R.dma("gpsimd",
                          lambda e, off=off, kc=kc, c0=c0, bw=bw, src=src, l=l: e.dma_start(
                              out=wdst(off, kc, bw),
                              in_=src[l][:, c0:c0 + bw].rearrange("(kc p) c -> p kc c", p=128)),
                          b, (), (), [b])

        convert_weights(0)

        pstage = talloc([128], F32, "pstage")
        prm = Buf("prm")
        pps = pbank(0)

        def colvec(name, src2d, n, width=None, scale=1.0):
            w = width or 128
            dst = alloc([n], F32, name)
            LD(pstage, pstage[0:n, 0:w], src2d)
            TR(pps[0:w, 0:n], pstage[0:n, 0:w], ident_f[0:n, 0:n], [pstage, ident_f], pps, True)
            A(lambda e: e.activation(out=dst[0:w, :], in_=pps[0:w, 0:n], func=AF.Copy, scale=scale),
              reads=[pps], writes=[dst])
            return dst

        sqD = math.sqrt(D)
        gcol = {}
        for l in range(DEPTH):
            for nm, src in (("mix", norm_mix_g), ("mlp", norm_mlp_g), ("ple", norm_ple_g)):
                gcol[(l, nm)] = colvec("g_%s%d" % (nm, l), src[l].rearrange("(n p) -> n p", p=128), 16, scale=sqD)
        gcol["final"] = colvec("g_final", final_norm_g.rearrange("(n p) -> n p", p=128), 16, scale=sqD)
        cb_col, cng_col, cnb_col, scb_col, cwT, scwT = {}, {}, {}, {}, {}, {}
        for l in range(DEPTH):
            cb_col[l] = colvec("cb%d" % l, conv_b[l].rearrange("(n p) -> n p", p=128), 4)
            cng_col[l] = colvec("cng%d" % l, conv_norm_g[l].rearrange("(n p) -> n p", p=128), 4)
            cnb_col[l] = colvec("cnb%d" % l, conv_norm_b[l].rearrange("(n p) -> n p", p=128), 4)
            scb_col[l] = colvec("scb%d" % l, ssd_conv_b[l].rearrange("(n p) -> n p", p=128), 10)
            cwT[l] = alloc([4, CONV_W], F32, "cwT%d" % l)
            for c in range(4):
                LD(pstage, pstage[0:CONV_W, :], conv_w[l][:, c * 128:(c + 1) * 128])
                TR(pps[:, 0:CONV_W], pstage[0:CONV_W, :], ident_f[0:CONV_W, 0:CONV_W], [pstage, ident_f], pps, True)
                A(lambda e, c=c, l=l: e.activation(out=cwT[l][:, c, :], in_=pps[:, 0:CONV_W], func=AF.Copy),
                  reads=[pps], adds=[cwT[l]])
            scwT[l] = alloc([10, SSD_CW], F32, "scwT%d" % l)
            for c in range(10):
                LD(pstage, pstage[0:SSD_CW, :], ssd_conv_w[l][:, c * 128:(c + 1) * 128])
                TR(pps[:, 0:SSD_CW], pstage[0:SSD_CW, :], ident_f[0:SSD_CW, 0:SSD_CW], [pstage, ident_f], pps, True)
                A(lambda e, c=c, l=l: e.activation(out=scwT[l][:, c, :], in_=pps[:, 0:SSD_CW], func=AF.Copy),
                  reads=[pps], adds=[scwT[l]])

        def bcast(name, src1d, n, temp=False):
            t = (talloc if temp else alloc)([n], F32, name)
            LD(t, t.ap, src1d.partition_broadcast(128))
            return t

        dtb_b, A_b, D_b, gss_b, gsub_col, neglam, scb_row = {}, {}, {}, {}, {}, {}, {}
        for l in range(DEPTH):
            lam_init = 0.8 - 0.6 * math.exp(-0.3 * l)
            dtb_b[l] = bcast("dtb%d" % l, ssd_dt_bias[l], 24)
            al = bcast("alog%d" % l, ssd_a_log[l], 24, temp=True)
            A_b[l] = alloc([24], F32, "A_b%d" % l)
            A(lambda e, al=al, l=l: e.activation(out=A_b[l].ap, in_=al.ap, func=AF.Exp), reads=[al], writes=[A_b[l]])
            V(lambda e, l=l: e.tensor_scalar_mul(out=A_b[l].ap, in0=A_b[l].ap, scalar1=-1.0), reads=[A_b[l]],
              writes=[A_b[l]])
            D_b[l] = bcast("D_b%d" % l, ssd_d[l], SSD_H)
            gss_b[l] = bcast("gss%d" % l, ssd_norm_g[l], SSD_W)
            V(lambda e, l=l: e.tensor_scalar_mul(out=gss_b[l].ap, in0=gss_b[l].ap, scalar1=math.sqrt(SSD_W)),
              reads=[gss_b[l]], writes=[gss_b[l]])
            gsub_col[l] = colvec("gsubc%d" % l, subln_g[l].rearrange("(o n) -> o n", o=1), 1,
                                 scale=math.sqrt(128.0) * (1.0 - lam_init))
            lq = [bcast("lam%d_%d" % (l, i), lam_in[i][l], 64, temp=True) for i in range(4)]
            s12 = talloc([2], F32, "s12_%d" % l)
            junk = talloc([64], F32, "junk%d" % l)
            for i in range(2):
                V(lambda e, i=i, lq=lq, junk=junk: e.tensor_tensor(out=junk.ap, in0=lq[2 * i].ap, in1=lq[2 * i + 1].ap,
                                                                   op=ALU.mult), reads=[lq[2 * i], lq[2 * i + 1]],
                  writes=[junk])
                V(lambda e, i=i, junk=junk, s12=s12: e.reduce_sum(out=s12[:, i:i + 1], in_=junk.ap,
                                                                  axis=mybir.AxisListType.X),
                  reads=[junk], adds=[s12])
            e12 = talloc([2], F32, "e12_%d" % l)
            A(lambda e, s12=s12, e12=e12: e.activation(out=e12.ap, in_=s12.ap, func=AF.Exp), reads=[s12], writes=[e12])
            neglam[l] = alloc([1], F32, "neglam%d" % l)
            V(lambda e, e12=e12, l=l: e.tensor_tensor(out=neglam[l].ap, in0=e12[:, 1:2], in1=e12[:, 0:1],
                                                      op=ALU.subtract), reads=[e12], writes=[neglam[l]])
            V(lambda e, l=l, li=lam_init: e.tensor_scalar_add(out=neglam[l].ap, in0=neglam[l].ap, scalar1=-li),
              reads=[neglam[l]], writes=[neglam[l]])
            rowf = talloc([1024], F32, "scbrowf%d" % l)
            LD(rowf, rowf[0:1, :], ssd_conv_b[l][0:1024].rearrange("(o n) -> o n", o=1))
            scb_row[l] = alloc([1024], BF16, "scbrow%d" % l)
            V(lambda e, l=l, rowf=rowf: e.tensor_copy(out=scb_row[l][0:1, :], in_=rowf[0:1, :]), reads=[rowf],
              writes=[scb_row[l]])
        cfar = alloc([2, ATT_H], F32, "cfar")
        LD(cfar, cfar[:, 0, :], rel_bias[15].partition_broadcast(128), mode="a")
        LD(cfar, cfar[:, 1, :], rel_bias[31].partition_broadcast(128), mode="a")

        rb_sb = talloc([ATT_H], F32, "rb_sb")
        oh_sb = talloc([NBIAS], F32, "oh_sb")
        wv_sb = talloc([NBIAS], F32, "wv_sb")
        LD(rb_sb, rb_sb[0:32, :], rel_bias)
        LD(oh_sb, oh_sb[0:32, :], onehot)
        for i, (c0, cw) in enumerate(((0, 512), (512, 512), (1024, NBIAS - 1024))):
            MM(pps[0:ATT_H, 0:cw], rb_sb[0:32, :], oh_sb[0:32, c0:c0 + cw], True, True, [rb_sb, oh_sb], pps, True)
            A(lambda e, c0=c0, cw=cw: e.activation(out=wv_sb[0:ATT_H, c0:c0 + cw], in_=pps[0:ATT_H, 0:cw], func=AF.Copy),
              reads=[pps], adds=[wv_sb])
        ST(wv_sb, wvd, wv_sb[0:ATT_H, :], DB("wv"), mode="w")

        st["gbase"] = st["off"]
        R.barrier()

        class WRing:
            def __init__(self, nslots, order, lag=0):
                self.lag = lag
                self.slots = [alloc([8192], BF16, "wslot%d" % i) for i in range(nslots)]
                self.order = order
                self.nl = 0
                self.nu = 0

            def _load(self, i):
                l, name, bi = self.order[i]
                off, kc, c0, bw = wspec[(l, name)][1][bi]
                sl = self.slots[i % len(self.slots)]
                view = sl.ap[:, 0:kc * bw].rearrange("p (k c) -> p k c", k=kc)
                LD(sl, view, wdst(off, kc, bw), reads=[wbuf[(l, name, bi)]])

            def get(self):
                i = self.nu
                while self.nl < len(self.order) and self.nl <= i + len(self.slots) - 1 - self.lag:
                    self._load(self.nl)
                    self.nl += 1
                self.nu += 1
                l, name, bi = self.order[i]
                off, kc, c0, bw = wspec[(l, name)][1][bi]
                sl = self.slots[i % len(self.slots)]
                return Tl(sl.ap[:, 0:kc * bw].rearrange("p (k c) -> p k c", k=kc), sl.buf), c0, bw

        ev_ctr = {"n": 0}

        def evac_engine():
            ev_ctr["n"] += 1
            return "scalar" if ev_ctr["n"] % 2 == 0 else "vector"

        def copy_out(out_ap, in_ap, reads, writes=(), adds=(), scale=None, eng=None):
            eng = eng or evac_engine()
            if eng == "scalar":
                if scale is None:
                    return A(lambda e: e.activation(out=out_ap, in_=in_ap, func=AF.Copy), reads, writes, adds)
                return A(lambda e: e.activation(out=out_ap, in_=in_ap, func=AF.Copy, scale=scale), reads, writes, adds)
            if scale is None:
                return V(lambda e: e.tensor_copy(out=out_ap, in_=in_ap), reads, writes, adds)
            return V(lambda e: e.tensor_scalar_mul(out=out_ap, in0=in_ap, scalar1=scale), reads, writes, adds)

        def rmsnorm_fm(hT, gc, dst, sq, ps_ssq, rs, sd):
            for q in range(4):
                A(lambda e, q=q: e.activation(out=sq[:, 4 * q:4 * q + 4, :], in_=hT[:, 4 * q:4 * q + 4, :], func=AF.Square),
                  reads=[hT], adds=[sq])
            for dc in range(NDC):
                MM(ps_ssq.ap, ones_b.ap, sq[:, dc, :], dc == 0, dc == NDC - 1, [ones_b, sq], ps_ssq, dc == NDC - 1)
            A(lambda e: e.activation(out=sd.ap, in_=ps_ssq.ap, func=AF.Sqrt, bias=eps_t[(D)][:, 0:1], scale=1.0),
              reads=[ps_ssq, eps_t[D]], writes=[sd])
            V(lambda e: e.reciprocal(out=rs.ap, in_=sd.ap), reads=[sd], writes=[rs])
            for dc in range(NDC):
                V(lambda e, dc=dc: e.scalar_tensor_tensor(out=dst[:, dc, :], in0=hT[:, dc, :], scalar=gc[:, dc:dc + 1],
                                                          in1=rs.ap, op0=ALU.mult, op1=ALU.mult),
                  reads=[hT, gc, rs], adds=[dst])

        eps_t = {}
        st["off"] = st["gbase"]
        for nfeat in (D, SSD_W, 128, 1):
            t_ = alloc([1], F32, "eps%d" % nfeat)
            V(lambda e, t_=t_, nfeat=nfeat: e.memset(t_.ap, float(nfeat) * EPS), writes=[t_])
            eps_t[nfeat] = t_
        st["gbase"] = st["off"]

        def make_norm(ps_ssq, sqr, rs, lnv):
            stt_ = {"fed": [], "n": 0}

            def feed(src_ap, src_tl):
                i = stt_["n"]
                stt_["n"] += 1
                sq = sqr[i % len(sqr)]
                A(lambda e: e.activation(out=sq.ap, in_=src_ap, func=AF.Square), reads=[src_tl], writes=[sq])
                stt_["fed"].append((i, sq))

            def pump(flush=False):
                while stt_["fed"] and (flush or len(stt_["fed"]) > 1):
                    i, sq = stt_["fed"].pop(0)
                    MM(ps_ssq.ap, ones_b.ap, sq.ap, i == 0, i == NDC - 1, [ones_b, sq], ps_ssq, True)

            def finish(gc, src, dst_fn):
                pump(flush=True)
                assert stt_["n"] == NDC
                stt_["n"] = 0
                A(lambda e: e.activation(out=lnv.ap, in_=ps_ssq.ap, func=AF.Ln, bias=eps_t[D][:, 0:1], scale=1.0),
                  reads=[ps_ssq, eps_t[D]], writes=[lnv])
                A(lambda e: e.activation(out=rs.ap, in_=lnv.ap, func=AF.Exp, scale=-0.5), reads=[lnv], writes=[rs])
                for dc in range(NDC):
                    dap, dtl = dst_fn(dc)
                    V(lambda e, dc=dc, dap=dap: e.scalar_tensor_tensor(out=dap, in0=src[:, dc, :], scalar=gc[:, dc:dc + 1],
                                                                       in1=rs.ap, op0=ALU.mult, op1=ALU.mult),
                      reads=[src, gc, rs], writes=[dtl])

            return feed, pump, finish

        def phase_p1(l):
            reset_arena()
            order = []
            for (sname, S, base) in seqs:
                for ti in range(S // T):
                    order += [(l, "in", bi) for bi in range(len(W_IN_BLOCKS))]
            ring = WRing(4, order, lag=1)
            hT = alloc([NDC, T], F32, "hT")
            sq_off = st["off"]
            sq = alloc([NDC, T], BF16, "sq")
            uT = alloc([NDC, T], BF16, "uT")
            uTk = [Tl(uT[:, kc, :], Buf("uT%d" % kc)) for kc in range(NDC)]
            sqr = [alloc([T], BF16, "sqr%d" % i) for i in range(3)]
            rs = alloc([T], F32, "rs")
            lnv = alloc([T], F32, "lnv")
            sig = alloc([T], F32, "sig")
            stg_glu = alloc([4, T], BF16, "stg_glu")
            stg_z = alloc([4, 768], BF16, "stg_z")
            stg_x = alloc([10, T], BF16, "stg_x")
            stg_dt = alloc([4, 24], F32, "stg_dt")
            tdt = alloc([4, 24], F32, "tdt")
            stg_q = alloc([6, T], BF16, "stg_q")
            stg_k = alloc([6, T], BF16, "stg_k")
            stg_v = alloc([4, 768], BF16, "stg_v")
            ps_ssq = pbank(0)
            banks = [pbank(i) for i in range(1, 8)]
            bk = {"i": 0}

            def nb():
                b = banks[bk["i"] % len(banks)]
                bk["i"] += 1
                return b

            feed, pump, finish = make_norm(ps_ssq, sqr, rs, lnv)

            def fm_chunk(w, c, ps):
                for kc in range(NDC):
                    MM(ps.ap, w[:, kc, c * 128:(c + 1) * 128], uTk[kc].ap, kc == 0, kc == NDC - 1, [w, uTk[kc]], ps,
                       kc == NDC - 1)

            def tm_sub(w, bw, sub, ps):
                for kc in range(NDC):
                    MM(ps[:, 0:bw], uTk[kc][:, sub * 128:(sub + 1) * 128], w[:, kc, 0:bw], kc == 0, kc == NDC - 1,
                       [w, uTk[kc]], ps, kc == NDC - 1)

            tiles = [(sname, S, base, ti) for (sname, S, base) in seqs for ti in range(S // T)]

            def load_h(tl_):
                sname_, S_, base_, ti_ = tl_
                t0_ = base_ + ti_ * T
                LD(hT, hT.ap, hTd[:, :, t0_:t0_ + T].rearrange("c p s -> p c s"), reads=[DB("hT", sname_, ti_)])

            if l > 0:
                load_h(tiles[0])

            for it, (sname, S, base, ti) in enumerate(tiles):
                if True:
                    t0 = base + ti * T
                    if l == 0:
                        xt_ap = arena[:, sq_off:sq_off + 4 * D].rearrange("p (s d) -> p s d", s=4)
                        R.dma("sync", lambda e, sname=sname, ti=ti, xt_ap=xt_ap: e.dma_start(
                            out=xt_ap, in_=x_in[sname][ti * T:(ti + 1) * T, :].rearrange("(s p) d -> p s d", p=128)),
                            sq.buf, (), [sq.buf] + [t.buf for t in uTk], ())
                        for dc in range(NDC):
                            ps = nb()
                            for sub in range(4):
                                TR(ps[:, sub * 128:(sub + 1) * 128], xt_ap[:, sub, dc * 128:(dc + 1) * 128], ident_f.ap,
                                   [sq, ident_f] + uTk, ps, sub == 3)
                            copy_out(hT[:, dc, :], ps.ap, [ps], adds=[hT])
                        ST(hT, hTd[:, :, t0:t0 + T].rearrange("c p s -> p c s"), hT.ap, DB("hT", sname, ti), mode="w")
                    for dc in range(NDC):
                        feed(hT[:, dc, :], hT)
                        pump()
                    finish(gcol[(l, "mix")], hT, lambda dc: (uTk[dc].ap, uTk[dc]))
                    if l > 0 and it + 1 < len(tiles):
                        load_h(tiles[it + 1])
                    wv_, _, _ = ring.get()
                    wg_, _, _ = ring.get()
                    pvs = [nb() for _ in range(4)]
                    for kc in range(NDC):
                        for c in range(4):
                            MM(pvs[c].ap, wv_[:, kc, c * 128:(c + 1) * 128], uTk[kc].ap, kc == 0, kc == NDC - 1,
                               [wv_, uTk[kc]], pvs[c], kc == NDC - 1)
                    for c in range(4):
                        pv = pvs[c]
                        pg = nb()
                        fm_chunk(wg_, c, pg)
                        A(lambda e, pg=pg: e.activation(out=sig.ap, in_=pg.ap, func=AF.Sigmoid), reads=[pg], writes=[sig])
                        V(lambda e, pv=pv, c=c: e.tensor_tensor(out=stg_glu[:, c, :], in0=pv.ap, in1=sig.ap, op=ALU.mult),
                          reads=[pv, sig], adds=[stg_glu])
                    ST(stg_glu, gluT[:, :, t0:t0 + T].rearrange("c p s -> p c s"), stg_glu.ap, DB("glu", sname, ti), mode="w")
                    for (zc0, zbw) in ((0, 512), (512, 256)):
                        w, _, bw = ring.get()
                        for sub in range(4):
                            ps = nb()
                            tm_sub(w, bw, sub, ps)
                            A(lambda e, ps=ps, sub=sub, zc0=zc0, bw=bw: e.activation(
                                out=stg_z[:, sub, zc0:zc0 + bw], in_=ps[:, 0:bw], func=AF.Silu), reads=[ps], adds=[stg_z])
                    ST(stg_z, zsd[t0:t0 + T, :].rearrange("(s p) c -> p s c", p=128), stg_z.ap, DB("zs", sname, ti), mode="w")
                    cc = 0
                    for nchunk in (4, 4, 2):
                        w, _, bw = ring.get()
                        for c in range(nchunk):
                            ps = nb()
                            fm_chunk(w, c, ps)
                            copy_out(stg_x[:, cc, :], ps.ap, [ps], adds=[stg_x])
                            cc += 1
                    ST(stg_x, xbcT[:, :, t0:t0 + T].rearrange("c p s -> p c s"), stg_x.ap, DB("xbc", sname, ti), mode="w")
                    w, _, bw = ring.get()
                    for sub in range(4):
                        ps = nb()
                        tm_sub(w, 24, sub, ps)
                        V(lambda e, ps=ps, sub=sub: e.tensor_tensor(out=tdt[:, sub, :], in0=ps[:, 0:24], in1=dtb_b[l].ap,
                                                                    op=ALU.add), reads=[ps, dtb_b[l]], adds=[tdt])
                    A(lambda e: e.activation(out=tdt.ap, in_=tdt.ap, func=AF.Exp), reads=[tdt], writes=[tdt])
                    A(lambda e: e.activation(out=stg_dt.ap, in_=tdt.ap, func=AF.Ln, bias=1.0), reads=[tdt], writes=[stg_dt])
                    ST(stg_dt, dtd[t0:t0 + T, :].rearrange("(s p) c -> p s c", p=128), stg_dt.ap, DB("dt", sname, ti), mode="w")
                    for (stg, dst, sc, nm) in ((stg_q, qTd, 0.125, "q"), (stg_k, kTd, None, "k")):
                        cc = 0
                        for nchunk in (4, 2):
                            w, _, bw = ring.get()
                            for c in range(nchunk):
                                ps = nb()
                                fm_chunk(w, c, ps)
                                copy_out(stg[:, cc, :], ps.ap, [ps], adds=[stg], scale=sc)
                                cc += 1
                        ST(stg, dst[:, :, t0:t0 + T].rearrange("c p s -> p c s"), stg.ap, DB(nm, sname, ti), mode="w")
                    for (vc0, vbw) in ((0, 512), (512, 256)):
                        w, _, bw = ring.get()
                        for sub in range(4):
                            ps = nb()
                            tm_sub(w, bw, sub, ps)
                            copy_out(stg_v[:, sub, vc0:vc0 + bw], ps[:, 0:bw], [ps], adds=[stg_v])
                    ST(stg_v, vd[t0:t0 + T, :].rearrange("(s p) c -> p s c", p=128), stg_v.ap, DB("v", sname, ti), mode="w")
            R.barrier()

        def phase_conv(l):
            reset_arena()
            dg = alloc([4, CONV_W, 128], BF16, "dg")
            for c in range(4):
                for j in range(CONV_W):
                    V(lambda e, c=c, j=j: e.tensor_scalar_mul(out=dg[:, c, j, :], in0=ident_f.ap,
                                                              scalar1=cwT[l][:, c, j:j + 1]),
                      reads=[ident_f, cwT[l]], adds=[dg])
            import os
            CUT = int(os.environ.get("CUT", "9"))
            if CUT <= 1:
                R.barrier()
                return
            HW = CONV_W // 2
            gins = [alloc([4, T + 2 * HW], BF16, "gin%d" % i) for i in range(2)]
            xc = alloc([4, T], F32, "xc")
            xcb = alloc([4, T], BF16, "xcb")
            sqb = alloc([4, T], BF16, "sqb")
            msq = alloc([T], F32, "msq")
            var = alloc([T], F32, "var")
            rstd = alloc([T], F32, "rstd")
            t1 = alloc([T], F32, "t1")
            ystg = alloc([4, T], BF16, "ystg")
            pcs = [pbank(i) for i in range(0, 4)]
            pmean = pbank(4)
            pex2 = pbank(5)
            it = 0
            for (sname, S, base) in seqs:
                nt = S // T
                for ti in range(nt):
                    t0 = base + ti * T
                    gin = gins[it % 2]
                    it += 1
                    lo = max(0, ti * T - HW)
                    hi = min(S, (ti + 1) * T + HW)
                    if ti == 0 or ti == nt - 1:
                        V(lambda e, gin=gin: e.memset(gin.ap, 0.0), writes=[gin])
                    o0 = lo - (ti * T - HW)
                    LD(gin, gin[:, :, o0:o0 + (hi - lo)], gluT[:, :, base + lo:base + hi].rearrange("c p s -> p c s"),
                       reads=[DB("glu", sname, tj) for tj in range(max(0, ti - 1), min(nt, ti + 2))])
                    for c in range(4):
                        ps = pcs[c]
                        for j in range(CONV_W):
                            MM(ps.ap, dg[:, c, j, :], gin[:, c, j:j + T], j == 0, j == CONV_W - 1, [dg, gin], ps,
                               j == CONV_W - 1)
                        A(lambda e, ps=ps, c=c: e.activation(out=xc[:, c, :], in_=ps.ap, func=AF.Identity,
                                                             bias=cb_col[l][:, c:c + 1], scale=1.0),
                          reads=[ps, cb_col[l]], adds=[xc])
                        A(lambda e, ps=ps, c=c: e.activation(out=sqb[:, c, :], in_=ps.ap, func=AF.Square,
                                                             bias=cb_col[l][:, c:c + 1], scale=1.0),
                          reads=[ps, cb_col[l]], adds=[sqb])
                    if CUT <= 2:
                        continue
                    V(lambda e: e.tensor_copy(out=xcb.ap, in_=xc.ap), reads=[xc], writes=[xcb])
                    for c in range(4):
                        MM(pmean.ap, ones512_b.ap, xcb[:, c, :], c == 0, c == 3, [ones512_b, xcb], pmean, c == 3)
                    for c in range(4):
                        MM(pex2.ap, ones512_b.ap, sqb[:, c, :], c == 0, c == 3, [ones512_b, sqb], pex2, c == 3)
                    A(lambda e: e.activation(out=msq.ap, in_=pmean.ap, func=AF.Square), reads=[pmean], writes=[msq])
                    V(lambda e: e.tensor_tensor(out=var.ap, in0=pex2.ap, in1=msq.ap, op=ALU.subtract), reads=[pex2, msq],
                      writes=[var])
                    V(lambda e: e.tensor_scalar_max(out=var.ap, in0=var.ap, scalar1=0.0), reads=[var], writes=[var])
                    A(lambda e: e.activation(out=var.ap, in_=var.ap, func=AF.Sqrt, bias=eps_t[1][:, 0:1], scale=1.0),
                      reads=[var, eps_t[1]], writes=[var])
                    V(lambda e: e.reciprocal(out=rstd.ap, in_=var.ap), reads=[var], writes=[rstd])
                    if CUT <= 3:
                        continue
                    for c in range(4):
                        V(lambda e, c=c: e.tensor_tensor(out=t1.ap, in0=xc[:, c, :], in1=pmean.ap, op=ALU.subtract),
                          reads=[xc, pmean], writes=[t1])
                        V(lambda e: e.tensor_tensor(out=t1.ap, in0=t1.ap, in1=rstd.ap, op=ALU.mult), reads=[t1, rstd],
                          writes=[t1])
                        A(lambda e, c=c: e.activation(out=ystg[:, c, :], in_=t1.ap, func=AF.Silu,
                                                      bias=cnb_col[l][:, c:c + 1], scale=cng_col[l][:, c:c + 1]),
                          reads=[t1, cnb_col[l], cng_col[l]], adds=[ystg])
                    ST(ystg, mixT[0:4, :, t0:t0 + T].rearrange("c p s -> p c s"), ystg.ap, DB("mixc", sname, ti), mode="w")
            R.barrier()

        def phase_ssd(l):
            reset_arena()
            if l + 1 < nlayers:
                convert_weights(l + 1)
            dgs = alloc([10, SSD_CW, 128], BF16, "dgs")
            for c in range(10):
                for j in range(SSD_CW):
                    V(lambda e, c=c, j=j: e.tensor_scalar_mul(out=dgs[:, c, j, :], in0=ident_f.ap,
                                                              scalar1=scwT[l][:, c, j:j + 1]),
                      reads=[ident_f, scwT[l]], adds=[dgs])
            xins = [alloc([10, T + 4], BF16, "xin%d" % i) for i in range(2)]
            xs4s = [alloc([4, 768], F32, "xs4_%d" % i) for i in range(2)]
            bt4s = [alloc([4, 256], BF16, "bt4_%d" % i) for i in range(2)]
            bcts = [alloc([4, T], BF16, "bct%d" % i) for i in range(2)]
            acss = [alloc([SSD_H], F32, "acs%d" % i) for i in range(2)]
            UA = alloc([SSD_H, 128], F32, "UA")
            expRs = [alloc([SSD_H, 128], F32, "expR%d" % i) for i in range(2)]
            Dms = [alloc([SSD_H, 128], F32, "Dm%d" % i) for i in range(2)]
            CBm = alloc([2, 128], F32, "CBm")
            MTs = [alloc([SSD_H, 128], BF16, "MT%d" % i) for i in range(2)]
            CdTs = [alloc([SSD_H, 128], BF16, "CdT%d" % i) for i in range(2)]
            xdts = [alloc([768], BF16, "xdt%d" % i) for i in range(2)]
            xdecs = [alloc([768], BF16, "xdec%d" % i) for i in range(2)]
            dtdec = alloc([SSD_H], F32, "dtdec")
            prev = alloc([768], F32, "prev")
            prev_bs = [alloc([768], BF16, "prevb%d" % i) for i in range(2)]
            tmp = alloc([768], F32, "tmp")
            tmp2 = alloc([768], F32, "tmp2")
            yfo = [alloc([768], F32, "yfo%d" % i) for i in range(2)]
            yfi = [alloc([768], F32, "yfi%d" % i) for i in range(2)]
            zsi = [alloc([768], BF16, "zsi%d" % i) for i in range(2)]
            ysum = alloc([768], F32, "ysum")
            junk = alloc([768], BF16, "junkb")
            ssq = alloc([1], F32, "ssq")
            lnv = alloc([1], F32, "lnv")
            rr = alloc([1], F32, "rr")
            ob = alloc([768], BF16, "ob")
            ystgs = [alloc([6, T], BF16, "ystg%d" % i) for i in range(2)]
            psX = Tl(psum[:, 0:2, :].rearrange("p a b -> p (a b)"), Buf("psX"))
            psF = pbank(2)
            psFb = Tl(psum[:, 2, :].bitcast(BF16), psF.buf)
            pRs = [pbank(3), pbank(4)]
            p5 = pbank(5)
            pY = Tl(psum[:, 6:8, :].rearrange("p a b -> p (a b)"), Buf("pY"))
            pacs = Tl(psum[:, 7, 400:412], Buf("pacs"))
            ctr = {"x": 0, "r": 0, "m": 0, "y": 0, "pb": 0, "fi": 0, "st": 0, "blk": 0}

            for (sname, S, base) in seqs:
                nch = S // 128
                nblk = S // T
                dt_all = alloc([nch, 24], F32, "dt_all_" + sname)
                a_all = alloc([nch, 24], F32, "a_all_" + sname)
                LD(dt_all, dt_all.ap, dtd[base:base + S, :].rearrange("(c p) k -> p c k", p=128),
                   reads=[DB("dt", sname, tj) for tj in range(nblk)])
                V(lambda e, a_all=a_all, dt_all=dt_all, nch=nch: e.tensor_tensor(
                    out=a_all.ap, in0=dt_all.ap, in1=A_b[l].ap.unsqueeze(1).broadcast_to([128, nch, 24]), op=ALU.mult),
                  reads=[dt_all, A_b[l]], writes=[a_all])
                for d in range(2):
                    Tri = U_f if d == 0 else L_f
                    lend = 127 if d == 0 else 0
                    V(lambda e: e.memset(prev.ap, 0.0), writes=[prev])
                    st_ = {"pb": prev_bs[ctr["pb"] % 2]}
                    ctr["pb"] += 1
                    V(lambda e, t_=st_["pb"]: e.memset(t_.ap, 0.0), writes=[st_["pb"]])
                    blk_order = list(range(nblk)) if d == 0 else list(range(nblk - 1, -1, -1))
                    ch_order = list(range(4)) if d == 0 else list(range(3, -1, -1))
                    chunks = [(bi, ch) for bi in blk_order for ch in ch_order]

                    def stageA(c, par, Tri=Tri, a_all=a_all, d=d):
                        a_v = a_all[:, c, d * 12:d * 12 + 12]
                        acs = acss[par]
                        expR = expRs[par]
                        Dm = Dms[par]
                        MM(pacs.ap, Tri.ap, a_v, True, True, [Tri, a_all], pacs, True)
                        V(lambda e: e.tensor_scalar_mul(out=acs.ap, in0=pacs.ap, scalar1=-1.0), reads=[pacs], writes=[acs])
                        V(lambda e: e.tensor_tensor(
                            out=UA.ap, in0=Tri.ap.unsqueeze(1).broadcast_to([128, SSD_H, 128]),
                            in1=a_v.unsqueeze(2).broadcast_to([128, SSD_H, 128]), op=ALU.mult),
                          reads=[Tri, a_all], writes=[UA])
                        for pc in range(3):
                            pR = pRs[ctr["r"] % 2]
                            ctr["r"] += 1
                            MM(pR.ap, ones_f.ap, UA[:, pc * 4:(pc + 1) * 4, :], True, True, [ones_f, UA], pR, True)
                            A(lambda e, pR=pR, pc=pc: e.activation(
                                out=expR[:, pc * 4:(pc + 1) * 4, :], in_=pR.ap.rearrange("p (h l) -> p h l", h=4),
                                func=AF.Exp), reads=[pR], adds=[expR])
                            for hh in range(4):
                                h = pc * 4 + hh
                                A(lambda e, pR=pR, hh=hh, h=h: e.activation(
                                    out=Dm[:, h, :], in_=pR[:, hh * 128:(hh + 1) * 128], func=AF.Exp,
                                    bias=acs[:, h:h + 1], scale=1.0), reads=[pR, acs], adds=[Dm])

                    xin_ready = {}

                    def load_xin(bi, sname=sname, S=S, base=base, nblk=nblk):
                        xin = xins[ctr["x"] % 2]
                        ctr["x"] += 1
                        lo = max(0, bi * T - 2)
                        hi = min(S, (bi + 1) * T + 2)
                        if bi == 0 or bi == nblk - 1:
                            V(lambda e, xin=xin: e.memset(xin.ap, 0.0), writes=[xin])
                        o0 = lo - (bi * T - 2)
                        LD(xin, xin[:, :, o0:o0 + (hi - lo)], xbcT[:, :, base + lo:base + hi].rearrange("c p s -> p c s"),
                           reads=[DB("xbc", sname, tj) for tj in range(max(0, bi - 1), min(nblk, bi + 2))])
                        return xin

                    def block_conv(bi, sname=sname, S=S, base=base, nblk=nblk):
                        if bi in xin_ready:
                            xin = xin_ready.pop(bi)
                        else:
                            xin = load_xin(bi)
                        k = ctr["blk"] % 2
                        ctr["blk"] += 1
                        xs4, bt4, bct = xs4s[k], bt4s[k], bcts[k]
                        for ch in range(4):
                            for cc in range(8):
                                first = (cc % 4 == 0)
                                for j in range(SSD_CW):
                                    MM(psX[:, cc * 128:(cc + 1) * 128], xin[:, cc, ch * 128 + j:ch * 128 + j + 128],
                                       dgs[:, cc, j, :], first and j == 0, False, [xin, dgs], psX, False,
                                       skip_group_check=True)
                                MM(psX[:, cc * 128:(cc + 1) * 128], ones_b[0:1, :], scb_row[l][0:1, cc * 128:(cc + 1) * 128],
                                   False, True, [ones_b, scb_row[l]], psX, cc == 7, skip_group_check=True)
                            A(lambda e, ch=ch: e.activation(out=xs4[:, ch, :], in_=psX[:, 0:768], func=AF.Silu),
                              reads=[psX], adds=[xs4])
                            A(lambda e, ch=ch: e.activation(out=bt4[:, ch, :], in_=psX[:, 768:1024], func=AF.Silu),
                              reads=[psX], adds=[bt4])
                        for k4 in range(4):
                            cc = 6 + k4
                            for j in range(SSD_CW):
                                MM(psF.ap, dgs[:, cc, j, :], xin[:, cc, j:j + T], j == 0, j == SSD_CW - 1, [dgs, xin], psF,
                                   j == SSD_CW - 1)
                            A(lambda e, k4=k4, cc=cc: e.activation(out=bct[:, k4, :], in_=psF.ap, func=AF.Silu,
                                                                   bias=scb_col[l][:, cc:cc + 1], scale=1.0),
                              reads=[psF, scb_col[l]], adds=[bct])
                        return xs4, bt4, bct

                    def stageB(bi, ch, par, blkbufs, ystg, sname=sname, base=base, d=d, lend=lend, Tri=Tri,
                               dt_all=dt_all):
                        xs4, bt4, bct = blkbufs
                        c = bi * 4 + ch
                        tc0 = base + bi * T + ch * 128
                        dt_v = dt_all[:, c, d * 12:d * 12 + 12]
                        expR = expRs[par]
                        Dm = Dms[par]
                        if d == 1:
                            yin = yfi[ctr["fi"] % 2]
                            zin = zsi[ctr["fi"] % 2]
                            ctr["fi"] += 1
                            LD(yin, yin.ap, yfd[tc0:tc0 + 128, :], reads=[DB("yf", sname, c)])
                            LD(zin, zin.ap, zsd[tc0:tc0 + 128, :], reads=[DB("zs", sname, bi)])
                        for g in range(2):
                            MM(p5[:, g * 128:(g + 1) * 128], bct[:, g, ch * 128:(ch + 1) * 128],
                               bct[:, 2 + g, ch * 128:(ch + 1) * 128], True, True, [bct], p5, g == 1)
                        V(lambda e: e.tensor_tensor(
                            out=CBm.ap, in0=p5[:, 0:256].rearrange("p (g l) -> p g l", g=2),
                            in1=Tri.ap.unsqueeze(1).broadcast_to([128, 2, 128]), op=ALU.mult),
                          reads=[p5, Tri], writes=[CBm])
                        MT = MTs[ctr["m"] % 2]
                        CdT = CdTs[ctr["m"] % 2]
                        xdt = xdts[ctr["m"] % 2]
                        xdec = xdecs[ctr["m"] % 2]
                        ctr["m"] += 1
                        for g in range(2):
                            V(lambda e, g=g: e.scalar_tensor_tensor(
                                out=MT[:, g * 6:(g + 1) * 6, :], in0=Dm[:, g * 6:(g + 1) * 6, :], scalar=1.0,
                                in1=CBm[:, g:g + 1, :].broadcast_to([128, 6, 128]), op0=ALU.min, op1=ALU.mult),
                              reads=[Dm, CBm], adds=[MT])
                        V(lambda e: e.tensor_tensor(
                            out=CdT.ap.rearrange("p (g r) l -> p g r l", g=2),
                            in0=expR.ap.rearrange("p (g r) l -> p g r l", g=2),
                            in1=bct[:, 2:4, ch * 128:(ch + 1) * 128].unsqueeze(2).broadcast_to([128, 2, 6, 128]),
                            op=ALU.mult), reads=[expR, bct], writes=[CdT])
                        V(lambda e: e.tensor_tensor(out=dtdec.ap, in0=dt_v, in1=Dm[:, :, lend], op=ALU.mult),
                          reads=[dt_all, Dm], writes=[dtdec])
                        xs_v = xs4[:, ch, :].rearrange("p (h q) -> p h q", h=SSD_H)
                        V(lambda e: e.tensor_tensor(
                            out=xdt.ap.rearrange("p (h q) -> p h q", h=SSD_H), in0=xs_v,
                            in1=dt_v.unsqueeze(2).broadcast_to([128, SSD_H, 64]), op=ALU.mult),
                          reads=[xs4, dt_all], writes=[xdt])
                        V(lambda e: e.tensor_tensor(
                            out=xdec.ap.rearrange("p (h q) -> p h q", h=SSD_H), in0=xs_v,
                            in1=dtdec.ap.unsqueeze(2).broadcast_to([128, SSD_H, 64]), op=ALU.mult),
                          reads=[xs4, dtdec], writes=[xdec])
                        pbcur = st_["pb"]
                        for h in range(SSD_H):
                            MM(pY[:, h * 64:(h + 1) * 64], MT[:, h, :], xdt[:, h * 64:(h + 1) * 64], h in (0, 8), False,
                               [MT, xdt], pY, False, skip_group_check=True)
                            MM(pY[:, h * 64:(h + 1) * 64], CdT[:, h, :], pbcur[:, h * 64:(h + 1) * 64], False, True,
                               [CdT, pbcur], pY, h == SSD_H - 1, skip_group_check=True)
                        for h in range(SSD_H):
                            g = h // 6
                            MM(psX[:, h * 64:(h + 1) * 64], bt4[:, ch, g * 128:(g + 1) * 128], xdec[:, h * 64:(h + 1) * 64],
                               h in (0, 8), True, [bt4, xdec], psX, h == SSD_H - 1, skip_group_check=True)
                        V(lambda e: e.tensor_tensor(
                            out=tmp.ap.rearrange("p (h q) -> p h q", h=SSD_H),
                            in0=prev.ap.rearrange("p (h q) -> p h q", h=SSD_H),
                            in1=expR[:, :, lend].unsqueeze(2).broadcast_to([128, SSD_H, 64]), op=ALU.mult),
                          reads=[prev, expR], writes=[tmp])
                        V(lambda e: e.tensor_tensor(out=prev.ap, in0=tmp.ap, in1=psX[:, 0:768], op=ALU.add),
                          reads=[tmp, psX], writes=[prev])
                        st_["pb"] = prev_bs[ctr["pb"] % 2]
                        ctr["pb"] += 1
                        A(lambda e, t_=st_["pb"]: e.activation(out=t_.ap, in_=prev.ap, func=AF.Copy), reads=[prev],
                          writes=[st_["pb"]])
                        if d == 0:
                            yo = yfo[ctr["y"] % 2]
                            ctr["y"] += 1
                            A(lambda e: e.activation(out=yo.ap, in_=pY[:, 0:768], func=AF.Copy), reads=[pY], writes=[yo])
                            ST(yo, yfd[tc0:tc0 + 128, :], yo.ap, DB("yf", sname, c), mode="w")
                        else:
                            V(lambda e: e.tensor_tensor(out=ysum.ap, in0=pY[:, 0:768], in1=yin.ap, op=ALU.add),
                              reads=[pY, yin], writes=[ysum])
                            V(lambda e: e.tensor_tensor(
                                out=tmp2.ap.rearrange("p (h q) -> p h q", h=SSD_H), in0=xs_v,
                                in1=D_b[l].ap.unsqueeze(2).broadcast_to([128, SSD_H, 64]), op=ALU.mult),
                              reads=[xs4, D_b[l]], writes=[tmp2])
                            V(lambda e: e.tensor_tensor(out=ysum.ap, in0=ysum.ap, in1=tmp2.ap, op=ALU.add),
                              reads=[ysum, tmp2], writes=[ysum])
                            V(lambda e: e.tensor_tensor(out=ysum.ap, in0=ysum.ap, in1=zin.ap, op=ALU.mult),
                              reads=[ysum, zin], writes=[ysum])
                            V(lambda e: e.scalar_tensor_tensor(out=junk.ap, in0=ysum.ap, scalar=1.0, in1=ysum.ap,
                                                               op0=ALU.mult, op1=ALU.mult, accum_out=ssq.ap),
                              reads=[ysum], writes=[junk, ssq])
                            A(lambda e: e.activation(out=lnv.ap, in_=ssq.ap, func=AF.Ln, bias=eps_t[SSD_W][:, 0:1], scale=1.0),
                              reads=[ssq, eps_t[SSD_W]], writes=[lnv])
                            A(lambda e: e.activation(out=rr.ap, in_=lnv.ap, func=AF.Exp, scale=-0.5), reads=[lnv], writes=[rr])
                            V(lambda e: e.scalar_tensor_tensor(out=ob.ap, in0=ysum.ap, scalar=rr[:, 0:1], in1=gss_b[l].ap,
                                                               op0=ALU.mult, op1=ALU.mult),
                              reads=[ysum, rr, gss_b[l]], writes=[ob])
                            for i6 in range(6):
                                TR(psFb[:, i6 * 128:(i6 + 1) * 128], ob[:, i6 * 128:(i6 + 1) * 128], ident_b.ap,
                                   [ob, ident_b], psF, i6 == 5)
                            A(lambda e: e.activation(
                                out=ystg[:, :, ch * 128:(ch + 1) * 128],
                                in_=psFb[:, 0:768].rearrange("p (c s) -> p c s", c=6), func=AF.Copy),
                              reads=[psF], adds=[ystg])

                    blkbufs = None
                    ystg = None
                    stageA(chunks[0][0] * 4 + chunks[0][1], 0)
                    for i, (bi, ch) in enumerate(chunks):
                        if i % 4 == 0:
                            blkbufs = block_conv(bi)
                            if i + 4 < len(chunks):
                                nbi = chunks[i + 4][0]
                                xin_ready[nbi] = load_xin(nbi)
                            ystg = ystgs[ctr["st"] % 2]
                            if d == 1:
                                ctr["st"] += 1
                        if i + 1 < len(chunks):
                            nb_, nc_ = chunks[i + 1]
                            stageA(nb_ * 4 + nc_, (i + 1) % 2)
                        stageB(bi, ch, i % 2, blkbufs, ystg)
                        if d == 1 and i % 4 == 3:
                            t0 = base + bi * T
                            ST(ystg, mixT[4:10, :, t0:t0 + T].rearrange("c p s -> p c s"), ystg.ap, DB("mixs", sname, bi),
                               mode="w")
            R.barrier()

        def phase_attn(l):
            reset_arena()
            Hb = alloc([ATT_H, 1152], BF16, "Hb")
            hstage = alloc([1152], F32, "hstage")
            for h in range(ATT_H):
                LD(hstage, hstage.ap, bass.AP(wvd.tensor, h * NBIAS, [[1, 128], [1, 1152]]), reads=[DB("wv")])
                V(lambda e, h=h: e.tensor_copy(out=Hb[:, h, :], in_=hstage.ap), reads=[hstage], adds=[Hb])
            SMAX = max(s[1] for s in seqs)
            qts = [alloc([SMAX], BF16, "qt%d" % i) for i in range(2)]
            kts = [alloc([SMAX], BF16, "kt%d" % i) for i in range(2)]
            vas = [alloc([SMAX // 128, 128], BF16, "va%d" % i) for i in range(2)]
            pTs = [alloc([2, T], BF16, "pT%d" % i) for i in range(3)]
            r1 = alloc([T], F32, "r1")
            t1 = alloc([T], F32, "t1a")
            r2 = alloc([T], F32, "r2")
            t2 = alloc([T], F32, "t2a")
            o_t = alloc([T], F32, "o_t")
            sqo = alloc([T], BF16, "sqo")
            sdv = alloc([T], F32, "sdv")
            ystgs = [alloc([T], BF16, "ystga%d" % i) for i in range(2)]
            zacc = [[alloc([T], F32, "zacc%d_%d" % (i, k)) for k in range(2)] for i in range(2)]
            import os
            ZENG = os.environ.get("ZENG", "vector,gpsimd").split(",")
            pSs = [Tl(psum[:, 0:2, :], Buf("pS0")), Tl(psum[:, 2:4, :], Buf("pS1"))]
            pN = [pbank(4), pbank(5)]
            pZ = [pbank(6), pbank(7)]
            ctr = {"hd": 0, "y": 0}
            heads = [(sname, S, base, h) for (sname, S, base) in seqs for h in range(ATT_H)]

            def load_head(idx):
                sname_, S_, base_, h_ = heads[idx]
                nt_ = S_ // T
                qt_, kt_, va_ = qts[idx % 2], kts[idx % 2], vas[idx % 2]
                LD(qt_, qt_[:, 0:S_], qTd[h_, :, base_:base_ + S_], reads=[DB("q", sname_, tj) for tj in range(nt_)])
                LD(kt_, kt_[:, 0:S_], kTd[h_, :, base_:base_ + S_], reads=[DB("k", sname_, tj) for tj in range(nt_)])
                LD(va_, va_[:, 0:S_ // 128, :],
                   vd[base_:base_ + S_, h_ * 128:(h_ + 1) * 128].rearrange("(j p) e -> p j e", p=128),
                   reads=[DB("v", sname_, tj) for tj in range(nt_)])
                return qt_, kt_, va_

            loaded = {0: load_head(0)}
            for hidx, (sname, S, base, h) in enumerate(heads):
                nk = S // 128
                nq = S // T
                ntile = S // T
                if True:
                    qt, kt, va = loaded.pop(hidx)
                    if hidx + 1 < len(heads):
                        loaded[hidx + 1] = load_head(hidx + 1)

                    def emit_S(Q, j, kt=kt, qt=qt, h=h):
                        step = Q * nk + j
                        pS = pSs[step % 2]
                        pT = pTs[step % 3]
                        dlt = 128 * j - 512 * Q
                        near = -128 <= dlt <= 512
                        for m in range(2):
                            MM(pS[:, m, :], kt[64 * m:64 * m + 64, j * 128:(j + 1) * 128],
                               qt[64 * m:64 * m + 64, Q * T:(Q + 1) * T], True, not near, [kt, qt], pS,
                               (m == 1) and not near)
                        if near:
                            off = 512 - dlt
                            for m in range(2):
                                MM(pS[:, m, :], J_b.ap, Hb[:, h, off:off + T], False, True, [J_b, Hb], pS, m == 1)
                            A(lambda e, pT=pT, pS=pS: e.activation(out=pT.ap, in_=pS.ap, func=AF.Exp), reads=[pS],
                              writes=[pT])
                        else:
                            side = 0 if dlt < 0 else 1
                            A(lambda e, pT=pT, pS=pS, side=side, h=h: e.activation(
                                out=pT.ap, in_=pS.ap, func=AF.Exp, bias=cfar[:, side, h:h + 1], scale=1.0),
                              reads=[pS, cfar], writes=[pT])

                    def emit_AV(Q, j, va=va):
                        step = Q * nk + j
                        pT = pTs[step % 3]
                        for m in range(2):
                            MM(pN[m].ap, va[:, j, :], pT[:, m, :], j == 0, j == nk - 1, [va, pT], pN[m], j == nk - 1)
                        za = zacc[0][j % 2]
                        if j < 2:
                            V(lambda e, pT=pT, za=za: e.tensor_copy(out=za.ap, in_=pT[:, 0, :]), reads=[pT], writes=[za])
                        else:
                            V(lambda e, pT=pT, za=za: e.tensor_tensor(out=za.ap, in0=za.ap, in1=pT[:, 0, :], op=ALU.add),
                              reads=[pT, za], writes=[za])
                        MM(pZ[1].ap, ones_b.ap, pT[:, 1, :], j == 0, j == nk - 1, [ones_b, pT], pZ[1], j == nk - 1)
                        if j == nk - 1:
                            MM(pZ[0].ap, ones_f.ap, zacc[0][0].ap, True, False, [ones_f, zacc[0][0]], pZ[0], False)
                            MM(pZ[0].ap, ones_f.ap, zacc[0][1].ap, False, True, [ones_f, zacc[0][1]], pZ[0], True)

                    emit_S(0, 0)
                    fin_q = []
                    for Q in range(nq):
                        ystg = ystgs[ctr["y"] % 2]
                        ctr["y"] += 1
                        for j in range(nk):
                            if j + 1 < nk:
                                emit_S(Q, j + 1)
                            elif Q + 1 < nq:
                                emit_S(Q + 1, 0)
                            if j == nk - 1:
                                while fin_q:
                                    fin_q.pop(0)()
                            emit_AV(Q, j)
                            if fin_q:
                                fin_q.pop(0)()
                        V(lambda e: e.tensor_copy(out=t1.ap, in_=pN[0].ap), reads=[pN[0]], writes=[t1])
                        V(lambda e: e.tensor_copy(out=t2.ap, in_=pN[1].ap), reads=[pN[1]], writes=[t2])
                        V(lambda e: e.tensor_copy(out=r2.ap, in_=pZ[1].ap), reads=[pZ[1]], writes=[r2])
                        V(lambda e: e.tensor_copy(out=r1.ap, in_=pZ[0].ap), reads=[pZ[0]], writes=[r1])

                        def f1():
                            V(lambda e: e.reciprocal(out=r1.ap, in_=r1.ap), reads=[r1], writes=[r1])

                        def f2():
                            V(lambda e: e.tensor_tensor(out=t1.ap, in0=t1.ap, in1=r1.ap, op=ALU.mult), reads=[t1, r1], writes=[t1])

                        def f3():
                            V(lambda e: e.reciprocal(out=r2.ap, in_=r2.ap), reads=[r2], writes=[r2])

                        def f4():
                            V(lambda e: e.tensor_tensor(out=t2.ap, in0=t2.ap, in1=r2.ap, op=ALU.mult), reads=[t2, r2], writes=[t2])

                        def f5():
                            V(lambda e: e.scalar_tensor_tensor(out=o_t.ap, in0=t2.ap, scalar=neglam[l][:, 0:1], in1=t1.ap,
                                                               op0=ALU.mult, op1=ALU.add), reads=[t2, t1, neglam[l]], writes=[o_t])

                        def f6():
                            V(lambda e: e.tensor_tensor(out=sqo.ap, in0=o_t.ap, in1=o_t.ap, op=ALU.mult), reads=[o_t], writes=[sqo])

                        def f7():
                            MM(pZ[0].ap, ones_b.ap, sqo.ap, True, True, [ones_b, sqo], pZ[0], True)

                        def f8():
                            A(lambda e: e.activation(out=sdv.ap, in_=pZ[0].ap, func=AF.Ln, bias=eps_t[128][:, 0:1], scale=1.0),
                              reads=[pZ[0], eps_t[128]], writes=[sdv])

                        def f9():
                            A(lambda e: e.activation(out=sdv.ap, in_=sdv.ap, func=AF.Exp, scale=-0.5), reads=[sdv], writes=[sdv])

                        def f10(ystg=ystg, Q=Q, h=h, sname=sname, base=base):
                            V(lambda e: e.scalar_tensor_tensor(out=ystg.ap, in0=o_t.ap, scalar=gsub_col[l][:, 0:1],
                                                               in1=sdv.ap, op0=ALU.mult, op1=ALU.mult),
                              reads=[o_t, gsub_col[l], sdv], writes=[ystg])
                            ST(ystg, mixT[10 + h, :, base + Q * T:base + (Q + 1) * T], ystg.ap, DB("mixa", sname, Q, h), mode="w")

                        fin_q = [f1, f2, f3, f4, f5, f6, f7, f8, f9, f10]
                    while fin_q:
                        fin_q.pop(0)()
            R.barrier()

        def phase_p3(l):
            reset_arena()
            last = (l == nlayers - 1)
            tiles = [(sname, S, base, ti) for (sname, S, base) in seqs for ti in range(S // T)]
            order = []
            for _ in tiles:
                order += [(l, "out", bi) for bi in range(4)]
                order += [(l, "up", bi) for bi in range(16)]
                order += [(l, "down", bi) for bi in range(16)]
                order += [(l, "gate", bi) for bi in range(4)]
            ring = WRing(3, order)
            wple = alloc([2, D], BF16, "wple")
            off, kc, c0, bw = wspec[(l, "ple")][1][0]
            LD(wple, wple.ap, wdst(off, kc, bw), reads=[wbuf[(l, "ple", 0)]])
            hT = alloc([NDC, T], F32, "hT")
            aT = alloc([NDC, T], BF16, "aT")
            aTk = [Tl(aT[:, kc, :], Buf("aT%d" % kc)) for kc in range(NDC)]
            hid_off = st["off"]
            hid = alloc([64, T], BF16, "hid")
            ystg = Tl(arena[:, hid_off:hid_off + 4 * D].rearrange("p (s d) -> p s d", s=4), hid.buf)
            sqr = [alloc([T], BF16, "sqr%d" % i) for i in range(3)]
            rs = alloc([T], F32, "rs")
            lnv = alloc([T], F32, "lnv")
            rl = alloc([T], F32, "rl")
            tmp = alloc([T], F32, "tmp3")
            pf = alloc([4, PLE], F32, "pf")
            pb = alloc([4, PLE], BF16, "pb")
            pTt = alloc([2, T], BF16, "pTt")
            ps_ssq = pbank(0)
            ps_pt = pbank(1)
            ps_ptb = Tl(psum[:, 1, :].bitcast(BF16), ps_pt.buf)
            banks = [pbank(i) for i in range(2, 8)]
            bk = {"i": 0}
            feed, pump, finish = make_norm(ps_ssq, sqr, rs, lnv)

            def nb():
                b = banks[bk["i"] % len(banks)]
                bk["i"] += 1
                return b

            def load_tile_inputs(tl_, what):
                sname, S, base, ti = tl_
                t0 = base + ti * T
                if what == "mix":
                    R.dma("sync", lambda e: e.dma_start(out=aT.ap, in_=mixT[:, :, t0:t0 + T].rearrange("c p s -> p c s")),
                          aTk[0].buf,
                          [DB("mixc", sname, ti), DB("mixs", sname, ti)] + [DB("mixa", sname, ti, h) for h in range(ATT_H)],
                          [t.buf for t in aTk], ())
                else:
                    LD(hT, hT.ap, hTd[:, :, t0:t0 + T].rearrange("c p s -> p c s"), reads=[DB("hT", sname, ti)])
                    LD(pf, pf.ap, p_in[sname][l, ti * T:(ti + 1) * T, :].rearrange("(s p) k -> p s k", p=128))

            def fm_block_kc_outer(w, n, evac):
                pss = [nb() for _ in range(n)]
                for kc in range(NDC):
                    for c in range(n):
                        MM(pss[c].ap, w[:, kc, c * 128:(c + 1) * 128], aTk[kc].ap, kc == 0, kc == NDC - 1, [w, aTk[kc]], pss[c],
                           kc == NDC - 1)
                for c in range(n):
                    evac(c, pss[c])

            def fm_chunk(w, c, ps):
                for kc in range(NDC):
                    MM(ps.ap, w[:, kc, c * 128:(c + 1) * 128], aTk[kc].ap, kc == 0, kc == NDC - 1, [w, aTk[kc]], ps,
                       kc == NDC - 1)

            load_tile_inputs(tiles[0], "mix")
            load_tile_inputs(tiles[0], "h")
            for it, (sname, S, base, ti) in enumerate(tiles):
                t0 = base + ti * T
                for bi in range(4):
                    w, _, _ = ring.get()
                    for c in range(4):
                        dc = bi * 4 + c
                        ps = nb()
                        fm_chunk(w, c, ps)
                        pump()
                        V(lambda e, ps=ps, dc=dc: e.tensor_tensor(out=hT[:, dc, :], in0=hT[:, dc, :], in1=ps.ap, op=ALU.add),
                          reads=[ps, hT], adds=[hT])
                        feed(hT[:, dc, :], hT)
                finish(gcol[(l, "mlp")], hT, lambda dc: (aTk[dc].ap, aTk[dc]))

                def evac_up(fc, ps):
                    A(lambda e, ps=ps: e.activation(out=rl.ap, in_=ps.ap, func=AF.Relu), reads=[ps], writes=[rl])
                    V(lambda e, ps=ps, fc=fc: e.tensor_tensor(out=hid[:, fc, :], in0=ps.ap, in1=rl.ap, op=ALU.mult),
                      reads=[ps, rl], adds=[hid])

                for bi in range(16):
                    w, _, _ = ring.get()
                    if bi == 0:
                        fm_block_kc_outer(w, 4, lambda c, ps: evac_up(c, ps))
                        continue
                    for c in range(4):
                        ps = nb()
                        fm_chunk(w, c, ps)
                        evac_up(bi * 4 + c, ps)
                for dc in range(NDC):
                    w, _, _ = ring.get()
                    ps = nb()
                    for kc in range(64):
                        MM(ps.ap, w[:, kc, :], hid[:, kc, :], kc == 0, kc == 63, [w, hid], ps, kc == 63)
                    pump()
                    V(lambda e, ps=ps, dc=dc: e.tensor_tensor(out=hT[:, dc, :], in0=hT[:, dc, :], in1=ps.ap, op=ALU.add),
                      reads=[ps, hT], adds=[hT])
                    feed(hT[:, dc, :], hT)
                V(lambda e: e.tensor_copy(out=pb.ap, in_=pf.ap), reads=[pf], writes=[pb])
                for kc in range(2):
                    for sub in range(4):
                        TR(ps_ptb[:, kc * T + sub * 128:kc * T + (sub + 1) * 128], pb[:, sub, kc * 128:(kc + 1) * 128],
                           ident_b.ap, [pb, ident_b], ps_pt, kc == 1 and sub == 3)
                A(lambda e: e.activation(out=pTt.ap, in_=ps_ptb.ap.rearrange("p (k s) -> p k s", k=2), func=AF.Copy),
                  reads=[ps_pt], writes=[pTt])
                finish(gcol[(l, "ple")], hT, lambda dc: (aTk[dc].ap, aTk[dc]))

                def evac_gate(dc, pg):
                    pp = nb()
                    for kc in range(2):
                        MM(pp.ap, wple[:, kc, dc * 128:(dc + 1) * 128], pTt[:, kc, :], kc == 0, kc == 1, [wple, pTt], pp,
                           kc == 1)
                    if last:
                        pump()
                    A(lambda e, pg=pg: e.activation(out=rl.ap, in_=pg.ap, func=AF.Sigmoid), reads=[pg], writes=[rl])
                    V(lambda e, pp=pp: e.tensor_tensor(out=tmp.ap, in0=pp.ap, in1=rl.ap, op=ALU.mult), reads=[pp, rl],
                      writes=[tmp])
                    V(lambda e, dc=dc: e.tensor_tensor(out=hT[:, dc, :], in0=hT[:, dc, :], in1=tmp.ap, op=ALU.add),
                      reads=[tmp, hT], adds=[hT])
                    if last:
                        feed(hT[:, dc, :], hT)

                for bi in range(4):
                    w, _, _ = ring.get()
                    if bi == 0:
                        fm_block_kc_outer(w, 4, lambda c, pg: evac_gate(c, pg))
                        continue
                    for c in range(4):
                        pg = nb()
                        fm_chunk(w, c, pg)
                        evac_gate(bi * 4 + c, pg)
                if it + 1 < len(tiles):
                    load_tile_inputs(tiles[it + 1], "mix")
                if not last:
                    ST(hT, hTd[:, :, t0:t0 + T].rearrange("c p s -> p c s"), hT.ap, DB("hT", sname, ti), mode="w")
                else:
                    finish(gcol["final"], hT, lambda dc: (hT[:, dc, :], hT))
                    for sub in range(4):
                        for d4 in range(4):
                            ps = nb()
                            for c in range(4):
                                dc = d4 * 4 + c
                                TR(ps[:, c * 128:(c + 1) * 128], hT[:, dc, sub * 128:(sub + 1) * 128], ident_f.ap,
                                   [hT, ident_f], ps, c == 3)
                            copy_out(ystg[:, sub, d4 * 512:(d4 + 1) * 512], ps.ap, [ps], adds=[ystg])
                    ST(ystg, y_out[sname][ti * T:(ti + 1) * T, :].rearrange("(s p) d -> p s d", p=128), ystg.ap,
                       DB("y", sname, ti), mode="w")
                if it + 1 < len(tiles):
                    load_tile_inputs(tiles[it + 1], "h")
            R.barrier()

        stop = False
        for l in range(nlayers):
            for nm, fn in (("p1", phase_p1), ("conv", phase_conv), ("ssd", phase_ssd), ("attn", phase_attn),
                           ("p3", phase_p3)):
                fn(l)
                if stop_after == (l, nm):
                    stop = True
                    break
            if stop:
                break

        finals = [(s, v) for (s, v) in R.pool]
        R.replay(finals)
    return nc


_CACHE = {}


def kernel(**inputs):
    SA, SB = 4096, 2048
    NCORES = 8
    key = (SA, SB)
    if key not in _CACHE:
        _CACHE[key] = build(SA, SB)
    nc = _CACHE[key]
    f32 = lambda a: np.ascontiguousarray(np.asarray(a, dtype=np.float32))
    xp = f32(inputs["x_prompt"])
    xs = f32(inputs["x_sample"])
    pp = f32(inputs["p_prompt"])
    psm = f32(inputs["p_sample"])
    shared = {
        "w_in": f32(inputs["w_in"]), "w_out": f32(inputs["w_out"]), "w_up": f32(inputs["w_up"]),
        "w_down": f32(inputs["w_down"]), "w_ple": f32(inputs["w_ple"]), "w_ple_gate": f32(inputs["w_ple_gate"]),
        "norm_mix_g": f32(inputs["norm_mix_g"]), "norm_mlp_g": f32(inputs["norm_mlp_g"]),
        "norm_ple_g": f32(inputs["norm_ple_g"]), "final_norm_g": f32(inputs["final_norm_g"]),
        "conv_w": f32(inputs["conv_w"]), "conv_b": f32(inputs["conv_b"]),
        "conv_norm_g": f32(inputs["conv_norm_g"]), "conv_norm_b": f32(inputs["conv_norm_b"]),
        "ssd_conv_w": f32(inputs["ssd_conv_w"]), "ssd_conv_b": f32(inputs["ssd_conv_b"]),
        "ssd_dt_bias": f32(inputs["ssd_dt_bias"]).reshape(DEPTH, 24),
        "ssd_a_log": f32(inputs["ssd_a_log"]).reshape(DEPTH, 24),
        "ssd_d": f32(inputs["ssd_d"]), "ssd_norm_g": f32(inputs["ssd_norm_g"]),
        "lambda_q1": f32(inputs["lambda_q1"]), "lambda_k1": f32(inputs["lambda_k1"]),
        "lambda_q2": f32(inputs["lambda_q2"]), "lambda_k2": f32(inputs["lambda_k2"]),
        "attn_subln_g": f32(inputs["attn_subln_g"]), "rel_bias": f32(inputs["rel_bias"]),
        "onehot": onehot_bias_table(),
    }
    in_maps = []
    for c in range(NCORES):
        m = dict(shared)
        m["xa"] = xs[c]
        m["pa"] = np.ascontiguousarray(psm[:, c])
        m["xb"] = xp[c % 4]
        m["pb"] = np.ascontiguousarray(pp[:, c % 4])
        in_maps.append(m)
    res = run_bass_kernel_spmd(nc, in_maps, core_ids=list(range(NCORES)))
    y_sample = np.stack([np.asarray(res.results[c]["ya"], dtype=np.float32) for c in range(NCORES)], axis=0)
    y_prompt = np.stack([np.asarray(res.results[c]["yb"], dtype=np.float32) for c in range(4)], axis=0)
    return (y_prompt, y_sample)
```

```python
import math
from contextlib import ExitStack

import numpy as np
import concourse.bass as bass
import concourse.mybir as mybir
from concourse.bass_utils import run_bass_kernel_spmd

F32 = mybir.dt.float32
BF16 = mybir.dt.bfloat16
ALU = mybir.AluOpType
AF = mybir.ActivationFunctionType

D = 2048
NDC = 16
DEPTH = 2
PLE = 256
EPS = 1e-6
CONV_CH = 512
CONV_W = 31
SSD_W = 768
SSD_H = 12
SSD_XBC = 1280
SSD_CW = 5
ATT_H = 6
D_FF = 8192
IN_COLS = 5400
T = 512
NBIAS = 1279

ENGS = ("sync", "scalar", "vector", "gpsimd", "tensor")


class Ev:
    __slots__ = ("sem", "val")

    def __init__(self, sem, val=None):
        self.sem = sem
        self.val = val


class Buf:
    __slots__ = ("name", "writers", "readers", "sem", "semcnt", "grp", "glob", "gen")

    def __init__(self, name, glob=False):
        self.name = name
        self.writers = []
        self.readers = []
        self.sem = None
        self.semcnt = 0
        self.grp = None
        self.glob = glob
        self.gen = []


class Rec:
    def __init__(self, nc, es):
        self.nc = nc
        self.es = es
        self.q = {e: [] for e in ENGS}
        self.cnt = {e: 0 for e in ENGS}
        self.esem = {e: es.enter_context(nc.semaphore("s_" + e)) for e in ENGS}
        self.pending = {e: [] for e in ENGS}
        self.pool = []
        self.local = []
        self.nsem = len(ENGS)
        self.bar = {e: [] for e in ENGS}

    def _deps(self, reads, writes, adds):
        deps = []
        for b in reads:
            deps += b.writers
            b.grp = None
        for b in writes:
            deps += b.writers
            deps += b.readers
            b.grp = None
        for b in adds:
            if b.readers:
                b.gen = list(b.readers) + list(b.writers)
            deps += b.gen
        return deps

    def _commit(self, ev, reads, writes, adds):
        for b in writes:
            b.writers = [ev]
            b.readers = []
            b.gen = []
        for b in adds:
            if b.readers:
                b.writers = [ev]
                b.readers = []
            else:
                if not b.writers or b.writers[-1] is not ev:
                    b.writers.append(ev)
                if len(b.writers) > 16:
                    b.writers = self._compact(b.writers)
        for b in reads:
            if not b.readers or b.readers[-1] is not ev:
                b.readers.append(ev)
            if len(b.readers) > 16:
                b.readers = self._compact(b.readers)

    @staticmethod
    def _compact(evs):
        best = {}
        keep = []
        for e in evs:
            if e.val is None:
                keep.append(e)
                continue
            k = id(e.sem)
            if k not in best or e.val > best[k].val:
                best[k] = e
        return keep + list(best.values())

    def op(self, eng, fn, reads=(), writes=(), adds=(), signal=True):
        deps = self._deps(reads, writes, adds) + self.bar[eng]
        self.bar[eng] = []
        ev = Ev(self.esem[eng])
        if signal:
            self.cnt[eng] += 1
            ev.val = self.cnt[eng]
            for p in self.pending[eng]:
                p.val = ev.val
            self.pending[eng] = []
        else:
            self.pending[eng].append(ev)
        self.q[eng].append((deps, fn, (self.esem[eng], 1) if signal else None, eng == "tensor"))
        self._commit(ev, reads, writes, adds)
        return ev

    def dma(self, eng, fn, home, reads=(), writes=(), adds=()):
        deps = self._deps(reads, writes, adds) + self.bar[eng]
        self.bar[eng] = []
        if home.sem is None:
            if self.pool and not home.glob:
                home.sem, home.semcnt = self.pool.pop()
            else:
                home.sem = self.es.enter_context(self.nc.semaphore("d%d" % self.nsem))
                home.semcnt = 0
                self.nsem += 1
            if not home.glob:
                self.local.append(home)
        home.semcnt += 16
        if home.grp is None:
            home.grp = Ev(home.sem)
        ev = home.grp
        ev.val = home.semcnt
        self.q[eng].append((deps, fn, (home.sem, 16), False))
        self._commit(ev, reads, writes, adds)
        home.grp = ev
        return ev

    def barrier(self):
        for e in ENGS:
            assert not self.pending[e], "unsignaled tail on " + e
        evs = [Ev(self.esem[e], self.cnt[e]) for e in ENGS if self.cnt[e] > 0]
        for b in self.local:
            evs.append(Ev(b.sem, b.semcnt))
            self.pool.append((b.sem, b.semcnt))
            b.sem = None
        self.local = []
        for e in ENGS:
            self.bar[e] = self.bar[e] + evs

    def replay(self, final_waits=()):
        nc = self.nc
        for e in ENGS:
            assert not self.pending[e], "unsignaled tail on " + e
        with nc.Block() as block:
            def run(engname, engobj):
                waited = {}
                for deps, fn, inc, is_pe in self.q[engname]:
                    need = {}
                    for d in deps:
                        if is_pe and d.sem is self.esem["tensor"]:
                            continue
                        k = id(d.sem)
                        if d.val > waited.get(k, 0) and d.val > need.get(k, (None, 0))[1]:
                            need[k] = (d.sem, d.val)
                    for k, (sem, val) in need.items():
                        engobj.wait_ge(sem, val)
                        waited[k] = val
                    ins = fn(engobj)
                    if inc is not None:
                        ins.then_inc(inc[0], inc[1])
                if engname == "sync":
                    for sem, val in final_waits:
                        engobj.wait_ge(sem, val)

            @block.sync
            def _(e):
                run("sync", e)

            @block.scalar
            def _(e):
                run("scalar", e)

            @block.vector
            def _(e):
                run("vector", e)

            @block.gpsimd
            def _(e):
                run("gpsimd", e)

            @block.tensor
            def _(e):
                run("tensor", e)


class Tl:
    __slots__ = ("ap", "buf")

    def __init__(self, ap, buf):
        self.ap = ap
        self.buf = buf

    def __getitem__(self, k):
        return self.ap[k]


def rel_bucket_np(rel):
    nb = 16
    max_exact = 8
    ret = np.where(rel > 0, nb, 0)
    n = np.abs(rel)
    nf = np.maximum(n, 1).astype(np.float32)
    large = max_exact + (np.log(nf / np.float32(max_exact)) / np.float32(math.log(128 / max_exact))
                         * np.float32(nb - max_exact)).astype(np.int32)
    large = np.minimum(large, nb - 1)
    return ret + np.where(n < max_exact, n, large)


def onehot_bias_table():
    s = np.arange(NBIAS)
    b = rel_bucket_np(639 - s)
    oh = np.zeros((32, NBIAS), np.float32)
    oh[b, s] = 1.0
    return oh


W_IN_BLOCKS = [(0, 512), (512, 512), (1024, 512), (1536, 256), (1792, 512), (2304, 512), (2816, 256),
               (3072, 24), (3096, 512), (3608, 256), (3864, 512), (4376, 256), (4632, 512), (5144, 256)]


def build(SA, SB, debug=False, stop_after=None, nlayers=DEPTH):
    nc = bass.Bass("TRN2", target_bir_lowering=False)
    STOT = SA + SB
    seqs = [("a", SA, 0), ("b", SB, SA)]

    def din(name, shape, dt=F32):
        return nc.dram_tensor(name, list(shape), dt, kind="ExternalInput").ap()

    def dint(name, shape, dt=F32):
        kind = "ExternalOutput" if (debug and name != "wsc") else "Internal"
        return nc.dram_tensor(name, list(shape), dt, kind=kind).ap()

    def dout(name, shape, dt=F32):
        return nc.dram_tensor(name, list(shape), dt, kind="ExternalOutput").ap()

    x_in = {"a": din("xa", [SA, D]), "b": din("xb", [SB, D])}
    p_in = {"a": din("pa", [DEPTH, SA, PLE]), "b": din("pb", [DEPTH, SB, PLE])}
    y_out = {"a": dout("ya", [SA, D]), "b": dout("yb", [SB, D])}
    w_in = din("w_in", [DEPTH, D, IN_COLS])
    w_out = din("w_out", [DEPTH, D, D])
    w_up = din("w_up", [DEPTH, D, D_FF])
    w_down = din("w_down", [DEPTH, D_FF, D])
    w_ple = din("w_ple", [DEPTH, PLE, D])
    w_gate = din("w_ple_gate", [DEPTH, D, D])
    norm_mix_g = din("norm_mix_g", [DEPTH, D])
    norm_mlp_g = din("norm_mlp_g", [DEPTH, D])
    norm_ple_g = din("norm_ple_g", [DEPTH, D])
    final_norm_g = din("final_norm_g", [D])
    conv_w = din("conv_w", [DEPTH, CONV_W, CONV_CH])
    conv_b = din("conv_b", [DEPTH, CONV_CH])
    conv_norm_g = din("conv_norm_g", [DEPTH, CONV_CH])
    conv_norm_b = din("conv_norm_b", [DEPTH, CONV_CH])
    ssd_conv_w = din("ssd_conv_w", [DEPTH, SSD_CW, SSD_XBC])
    ssd_conv_b = din("ssd_conv_b", [DEPTH, SSD_XBC])
    ssd_dt_bias = din("ssd_dt_bias", [DEPTH, 24])
    ssd_a_log = din("ssd_a_log", [DEPTH, 24])
    ssd_d = din("ssd_d", [DEPTH, SSD_H])
    ssd_norm_g = din("ssd_norm_g", [DEPTH, SSD_W])
    lam_in = [din(n, [DEPTH, 64]) for n in ("lambda_q1", "lambda_k1", "lambda_q2", "lambda_k2")]
    subln_g = din("attn_subln_g", [DEPTH, 128])
    rel_bias = din("rel_bias", [32, ATT_H])
    onehot = din("onehot", [32, NBIAS])

    hTd = dint("hTd", [NDC, 128, STOT])
    gluT = dint("gluT", [4, 128, STOT], BF16)
    xbcT = dint("xbcT", [10, 128, STOT], BF16)
    qTd = dint("qTd", [6, 128, STOT], BF16)
    kTd = dint("kTd", [6, 128, STOT], BF16)
    vd = dint("vd", [STOT, 768], BF16)
    zsd = dint("zsd", [STOT, 768], BF16)
    dtd = dint("dtd", [STOT, 24])
    yfd = dint("yfd", [STOT, 768])
    mixT = dint("mixT", [NDC, 128, STOT], BF16)
    wvd = dint("wvd", [ATT_H, NBIAS])

    wspec = {}
    woff = 0
    for l in range(DEPTH):
        for name, src, K, blocks in (
            ("in", w_in, D, W_IN_BLOCKS),
            ("out", w_out, D, [(i * 512, 512) for i in range(4)]),
            ("up", w_up, D, [(i * 512, 512) for i in range(16)]),
            ("down", w_down, D_FF, [(i * 128, 128) for i in range(16)]),
            ("gate", w_gate, D, [(i * 512, 512) for i in range(4)]),
            ("ple", w_ple, PLE, [(0, 2048)]),
        ):
            kc = K // 128
            lst = []
            for (c0, bw) in blocks:
                lst.append((woff, kc, c0, bw))
                woff += 128 * kc * bw
            wspec[(l, name)] = (src, lst)
    wsc = dint("wsc", [woff], BF16)

    def wdst(off, kc, bw):
        return bass.AP(wsc.tensor, off, [[kc * bw, 128], [bw, kc], [1, bw]])

    es = ExitStack()
    with es:
        R = Rec(nc, es)
        ARENA = 53200
        arena = es.enter_context(nc.sbuf_tensor("arena", [128, ARENA], F32))
        psum = es.enter_context(nc.psum_tensor("psum", [128, 8, 512], F32))
        st = {"off": 0, "gbase": 0, "n": 0, "top": ARENA}

        def alloc(shape, dt=F32, name=None, buf=None, glob=False):
            n = int(np.prod(shape))
            n32 = n if dt == F32 else (n + 1) // 2
            assert st["off"] + n32 <= ARENA, "SBUF arena overflow %d" % (st["off"] + n32)
            a = arena[:, st["off"]:st["off"] + n32]
            st["off"] += n32
            if dt != F32:
                a = a.bitcast(BF16)[:, 0:n]
            if len(shape) == 2:
                a = a.rearrange("p (a b) -> p a b", a=shape[0])
            elif len(shape) == 3:
                a = a.rearrange("p (a b c) -> p a b c", a=shape[0], b=shape[1])
            st["n"] += 1
            return Tl(a, buf if buf is not None else Buf(name or ("t%d" % st["n"]), glob=glob))

        def talloc(shape, dt=F32, name=None):
            n = int(np.prod(shape))
            n32 = n if dt == F32 else (n + 1) // 2
            st["top"] -= n32
            assert st["top"] >= st["off"], "arena overflow (temp)"
            a = arena[:, st["top"]:st["top"] + n32]
            if dt != F32:
                a = a.bitcast(BF16)[:, 0:n]
            if len(shape) == 2:
                a = a.rearrange("p (a b) -> p a b", a=shape[0])
            st["n"] += 1
            return Tl(a, Buf(name or ("tt%d" % st["n"])))

        def reset_arena():
            st["off"] = st["gbase"]

        def pbank(i, buf=None):
            return Tl(psum[:, i, :], buf if buf is not None else Buf("ps%d" % i))

        def bufs_of(lst):
            return [t.buf if isinstance(t, Tl) else t for t in lst]

        def V(fn, reads=(), writes=(), adds=(), eng="vector"):
            return R.op(eng, fn, bufs_of(reads), bufs_of(writes), bufs_of(adds))

        def A(fn, reads=(), writes=(), adds=()):
            return R.op("scalar", fn, bufs_of(reads), bufs_of(writes), bufs_of(adds))

        def G(fn, reads=(), writes=(), adds=()):
            return R.op("gpsimd", fn, bufs_of(reads), bufs_of(writes), bufs_of(adds))

        def MM(out, lhsT, rhs, start, stop, reads, acc, signal, **kw):
            return R.op("tensor",
                        lambda e: e.matmul(out, lhsT=lhsT, rhs=rhs, start=start, stop=stop, **kw),
                        bufs_of(reads), (), bufs_of([acc]), signal=signal)

        def TR(out, in_, ident, reads, acc, signal):
            return R.op("tensor", lambda e: e.transpose(out=out, in_=in_, identity=ident),
                        bufs_of(reads), (), bufs_of([acc]), signal=signal)

        def LD(out_tl, out_ap, in_ap, reads=(), mode="w", q="sync"):
            if mode == "w":
                return R.dma(q, lambda e: e.dma_start(out=out_ap, in_=in_ap), out_tl.buf,
                             bufs_of(reads), [out_tl.buf], ())
            return R.dma(q, lambda e: e.dma_start(out=out_ap, in_=in_ap), out_tl.buf,
                         bufs_of(reads), (), [out_tl.buf])

        def ST(src_tl, out_ap, in_ap, dbuf, q="sync", mode="a"):
            if mode == "a":
                return R.dma(q, lambda e: e.dma_start(out=out_ap, in_=in_ap), src_tl.buf,
                             [src_tl.buf], (), [dbuf])
            return R.dma(q, lambda e: e.dma_start(out=out_ap, in_=in_ap), src_tl.buf,
                         [src_tl.buf], [dbuf], ())

        dbufs = {}

        def DB(*key):
            if key not in dbufs:
                dbufs[key] = Buf("dr_" + "_".join(str(k) for k in key), glob=True)
            return dbufs[key]

        out_bufs = []

        ident_f = alloc([128], F32, "ident_f")
        ident_b = alloc([128], BF16, "ident_b")
        J_b = alloc([128], BF16, "J_b")
        U_f = alloc([128], F32, "U_f")
        L_f = alloc([128], F32, "L_f")
        ones_f = alloc([128], F32, "ones_f")
        ones_b = alloc([128], BF16, "ones_b")
        ones512_b = alloc([128], BF16, "ones512")
        tmpc = talloc([128], F32, "tmpc")

        G(lambda e: e.memset(ident_f.ap, 0.0), writes=[ident_f])
        G(lambda e: e.affine_select(out=ident_f.ap, in_=ident_f.ap, pattern=[[-1, 128]], compare_op=ALU.not_equal,
                                    fill=1.0, base=0, channel_multiplier=1), reads=[ident_f], writes=[ident_f])
        G(lambda e: e.memset(ones_f.ap, 1.0), writes=[ones_f])
        G(lambda e: e.memset(tmpc.ap, 0.0), writes=[tmpc])
        G(lambda e: e.affine_select(out=tmpc.ap, in_=tmpc.ap, pattern=[[1, 128]], compare_op=ALU.not_equal,
                                    fill=1.0, base=-127, channel_multiplier=1), reads=[tmpc], writes=[tmpc])
        G(lambda e: e.affine_select(out=U_f.ap, in_=ones_f.ap, pattern=[[1, 128]], compare_op=ALU.is_ge,
                                    fill=0.0, base=0, channel_multiplier=-1), reads=[ones_f], writes=[U_f])
        G(lambda e: e.affine_select(out=L_f.ap, in_=ones_f.ap, pattern=[[-1, 128]], compare_op=ALU.is_ge,
                                    fill=0.0, base=0, channel_multiplier=1), reads=[ones_f], writes=[L_f])
        V(lambda e: e.tensor_copy(out=ident_b.ap, in_=ident_f.ap), reads=[ident_f], writes=[ident_b])
        V(lambda e: e.tensor_copy(out=J_b.ap, in_=tmpc.ap), reads=[tmpc], writes=[J_b])
        V(lambda e: e.tensor_copy(out=ones_b.ap, in_=ones_f.ap), reads=[ones_f], writes=[ones_b])
        V(lambda e: e.tensor_scalar_mul(out=ones512_b.ap, in0=ones_f.ap, scalar1=1.0 / 512.0), reads=[ones_f],
          writes=[ones512_b])

        wbuf = {}
        for l in range(DEPTH):
            for name in ("in", "out", "up", "down", "gate", "ple"):
                src, lst = wspec[(l, name)]
                b = Buf("w_%d_%s" % (l, name), glob=True)
                for bi, (off, kc, c0, bw) in enumerate(lst):
                    if l == 0 and name == "in":
                        b = Buf("w_%d_%s_%d" % (l, name, bi), glob=True)
                    wbuf[(l, name, bi)] = b

        def convert_weights(l):
            for name in ("in", "out", "up", "down", "gate", "ple"):
                src, lst = wspec[(l, name)]
                for bi, (off, kc, c0, bw) in enumerate(lst):
                    b = wbuf[(l, name, bi)]
                    R.dma("gpsimd",
                          lambda e, off=off, kc=kc, c0=c0, bw=bw, src=src, l=l: e.dma_start(
                              out=wdst(off, kc, bw),
                              in_=src[l][:, c0:c0 + bw].rearrange("(kc p) c -> p kc c", p=128)),
                          b, (), (), [b])

        convert_weights(0)

        pstage = talloc([128], F32, "pstage")
        prm = Buf("prm")
        pps = pbank(0)

        def colvec(name, src2d, n, width=None, scale=1.0):
            w = width or 128
            dst = alloc([n], F32, name)
            LD(pstage, pstage[0:n, 0:w], src2d)
            TR(pps[0:w, 0:n], pstage[0:n, 0:w], ident_f[0:n, 0:n], [pstage, ident_f], pps, True)
            A(lambda e: e.activation(out=dst[0:w, :], in_=pps[0:w, 0:n], func=AF.Copy, scale=scale),
              reads=[pps], writes=[dst])
            return dst

        sqD = math.sqrt(D)
        gcol = {}
        for l in range(DEPTH):
            for nm, src in (("mix", norm_mix_g), ("mlp", norm_mlp_g), ("ple", norm_ple_g)):
                gcol[(l, nm)] = colvec("g_%s%d" % (nm, l), src[l].rearrange("(n p) -> n p", p=128), 16, scale=sqD)
        gcol["final"] = colvec("g_final", final_norm_g.rearrange("(n p) -> n p", p=128), 16, scale=sqD)
        cb_col, cng_col, cnb_col, scb_col, cwT, scwT = {}, {}, {}, {}, {}, {}
        for l in range(DEPTH):
            cb_col[l] = colvec("cb%d" % l, conv_b[l].rearrange("(n p) -> n p", p=128), 4)
            cng_col[l] = colvec("cng%d" % l, conv_norm_g[l].rearrange("(n p) -> n p", p=128), 4)
            cnb_col[l] = colvec("cnb%d" % l, conv_norm_b[l].rearrange("(n p) -> n p", p=128), 4)
            scb_col[l] = colvec("scb%d" % l, ssd_conv_b[l].rearrange("(n p) -> n p", p=128), 10)
            cwT[l] = alloc([4, CONV_W], F32, "cwT%d" % l)
            for c in range(4):
                LD(pstage, pstage[0:CONV_W, :], conv_w[l][:, c * 128:(c + 1) * 128])
                TR(pps[:, 0:CONV_W], pstage[0:CONV_W, :], ident_f[0:CONV_W, 0:CONV_W], [pstage, ident_f], pps, True)
                A(lambda e, c=c, l=l: e.activation(out=cwT[l][:, c, :], in_=pps[:, 0:CONV_W], func=AF.Copy),
                  reads=[pps], adds=[cwT[l]])
            scwT[l] = alloc([10, SSD_CW], F32, "scwT%d" % l)
            for c in range(10):
                LD(pstage, pstage[0:SSD_CW, :], ssd_conv_w[l][:, c * 128:(c + 1) * 128])
                TR(pps[:, 0:SSD_CW], pstage[0:SSD_CW, :], ident_f[0:SSD_CW, 0:SSD_CW], [pstage, ident_f], pps, True)
                A(lambda e, c=c, l=l: e.activation(out=scwT[l][:, c, :], in_=pps[:, 0:SSD_CW], func=AF.Copy),
                  reads=[pps], adds=[scwT[l]])

        def bcast(name, src1d, n, temp=False):
            t = (talloc if temp else alloc)([n], F32, name)
            LD(t, t.ap, src1d.partition_broadcast(128))
            return t

        dtb_b, A_b, D_b, gss_b, gsub_col, neglam, scb_row = {}, {}, {}, {}, {}, {}, {}
        for l in range(DEPTH):
            lam_init = 0.8 - 0.6 * math.exp(-0.3 * l)
            dtb_b[l] = bcast("dtb%d" % l, ssd_dt_bias[l], 24)
            al = bcast("alog%d" % l, ssd_a_log[l], 24, temp=True)
            A_b[l] = alloc([24], F32, "A_b%d" % l)
            A(lambda e, al=al, l=l: e.activation(out=A_b[l].ap, in_=al.ap, func=AF.Exp), reads=[al], writes=[A_b[l]])
            V(lambda e, l=l: e.tensor_scalar_mul(out=A_b[l].ap, in0=A_b[l].ap, scalar1=-1.0), reads=[A_b[l]],
              writes=[A_b[l]])
            D_b[l] = bcast("D_b%d" % l, ssd_d[l], SSD_H)
            gss_b[l] = bcast("gss%d" % l, ssd_norm_g[l], SSD_W)
            V(lambda e, l=l: e.tensor_scalar_mul(out=gss_b[l].ap, in0=gss_b[l].ap, scalar1=math.sqrt(SSD_W)),
              reads=[gss_b[l]], writes=[gss_b[l]])
            gsub_col[l] = colvec("gsubc%d" % l, subln_g[l].rearrange("(o n) -> o n", o=1), 1,
                                 scale=math.sqrt(128.0) * (1.0 - lam_init))
            lq = [bcast("lam%d_%d" % (l, i), lam_in[i][l], 64, temp=True) for i in range(4)]
            s12 = talloc([2], F32, "s12_%d" % l)
            junk = talloc([64], F32, "junk%d" % l)
            for i in range(2):
                V(lambda e, i=i, lq=lq, junk=junk: e.tensor_tensor(out=junk.ap, in0=lq[2 * i].ap, in1=lq[2 * i + 1].ap,
                                                                   op=ALU.mult), reads=[lq[2 * i], lq[2 * i + 1]],
                  writes=[junk])
                V(lambda e, i=i, junk=junk, s12=s12: e.reduce_sum(out=s12[:, i:i + 1], in_=junk.ap,
                                                                  axis=mybir.AxisListType.X),
                  reads=[junk], adds=[s12])
            e12 = talloc([2], F32, "e12_%d" % l)
            A(lambda e, s12=s12, e12=e12: e.activation(out=e12.ap, in_=s12.ap, func=AF.Exp), reads=[s12], writes=[e12])
            neglam[l] = alloc([1], F32, "neglam%d" % l)
            V(lambda e, e12=e12, l=l: e.tensor_tensor(out=neglam[l].ap, in0=e12[:, 1:2], in1=e12[:, 0:1],
                                                      op=ALU.subtract), reads=[e12], writes=[neglam[l]])
            V(lambda e, l=l, li=lam_init: e.tensor_scalar_add(out=neglam[l].ap, in0=neglam[l].ap, scalar1=-li),
              reads=[neglam[l]], writes=[neglam[l]])
            rowf = talloc([1024], F32, "scbrowf%d" % l)
            LD(rowf, rowf[0:1, :], ssd_conv_b[l][0:1024].rearrange("(o n) -> o n", o=1))
            scb_row[l] = alloc([1024], BF16, "scbrow%d" % l)
            V(lambda e, l=l, rowf=rowf: e.tensor_copy(out=scb_row[l][0:1, :], in_=rowf[0:1, :]), reads=[rowf],
              writes=[scb_row[l]])
        cfar = alloc([2, ATT_H], F32, "cfar")
        LD(cfar, cfar[:, 0, :], rel_bias[15].partition_broadcast(128), mode="a")
        LD(cfar, cfar[:, 1, :], rel_bias[31].partition_broadcast(128), mode="a")

        rb_sb = talloc([ATT_H], F32, "rb_sb")
        oh_sb = talloc([NBIAS], F32, "oh_sb")
        wv_sb = talloc([NBIAS], F32, "wv_sb")
        LD(rb_sb, rb_sb[0:32, :], rel_bias)
        LD(oh_sb, oh_sb[0:32, :], onehot)
        for i, (c0, cw) in enumerate(((0, 512), (512, 512), (1024, NBIAS - 1024))):
            MM(pps[0:ATT_H, 0:cw], rb_sb[0:32, :], oh_sb[0:32, c0:c0 + cw], True, True, [rb_sb, oh_sb], pps, True)
            A(lambda e, c0=c0, cw=cw: e.activation(out=wv_sb[0:ATT_H, c0:c0 + cw], in_=pps[0:ATT_H, 0:cw], func=AF.Copy),
              reads=[pps], adds=[wv_sb])
        ST(wv_sb, wvd, wv_sb[0:ATT_H, :], DB("wv"), mode="w")

        st["gbase"] = st["off"]
        R.barrier()

        class WRing:
            def __init__(self, nslots, order, lag=0):
                self.lag = lag
                self.slots = [alloc([8192], BF16, "wslot%d" % i) for i in range(nslots)]
                self.order = order
                self.nl = 0
                self.nu = 0

            def _load(self, i):
                l, name, bi = self.order[i]
                off, kc, c0, bw = wspec[(l, name)][1][bi]
                sl = self.slots[i % len(self.slots)]
                view = sl.ap[:, 0:kc * bw].rearrange("p (k c) -> p k c", k=kc)
                LD(sl, view, wdst(off, kc, bw), reads=[wbuf[(l, name, bi)]])

            def get(self):
                i = self.nu
                while self.nl < len(self.order) and self.nl <= i + len(self.slots) - 1 - self.lag:
                    self._load(self.nl)
                    self.nl += 1
                self.nu += 1
                l, name, bi = self.order[i]
                off, kc, c0, bw = wspec[(l, name)][1][bi]
                sl = self.slots[i % len(self.slots)]
                return Tl(sl.ap[:, 0:kc * bw].rearrange("p (k c) -> p k c", k=kc), sl.buf), c0, bw

        ev_ctr = {"n": 0}

        def evac_engine():
            ev_ctr["n"] += 1
            return "scalar" if ev_ctr["n"] % 2 == 0 else "vector"

        def copy_out(out_ap, in_ap, reads, writes=(), adds=(), scale=None, eng=None):
            eng = eng or evac_engine()
            if eng == "scalar":
                if scale is None:
                    return A(lambda e: e.activation(out=out_ap, in_=in_ap, func=AF.Copy), reads, writes, adds)
                return A(lambda e: e.activation(out=out_ap, in_=in_ap, func=AF.Copy, scale=scale), reads, writes, adds)
            if scale is None:
                return V(lambda e: e.tensor_copy(out=out_ap, in_=in_ap), reads, writes, adds)
            return V(lambda e: e.tensor_scalar_mul(out=out_ap, in0=in_ap, scalar1=scale), reads, writes, adds)

        def rmsnorm_fm(hT, gc, dst, sq, ps_ssq, rs, sd):
            for q in range(4):
                A(lambda e, q=q: e.activation(out=sq[:, 4 * q:4 * q + 4, :], in_=hT[:, 4 * q:4 * q + 4, :], func=AF.Square),
                  reads=[hT], adds=[sq])
            for dc in range(NDC):
                MM(ps_ssq.ap, ones_b.ap, sq[:, dc, :], dc == 0, dc == NDC - 1, [ones_b, sq], ps_ssq, dc == NDC - 1)
            A(lambda e: e.activation(out=sd.ap, in_=ps_ssq.ap, func=AF.Sqrt, bias=eps_t[(D)][:, 0:1], scale=1.0),
              reads=[ps_ssq, eps_t[D]], writes=[sd])
            V(lambda e: e.reciprocal(out=rs.ap, in_=sd.ap), reads=[sd], writes=[rs])
            for dc in range(NDC):
                V(lambda e, dc=dc: e.scalar_tensor_tensor(out=dst[:, dc, :], in0=hT[:, dc, :], scalar=gc[:, dc:dc + 1],
                                                          in1=rs.ap, op0=ALU.mult, op1=ALU.mult),
                  reads=[hT, gc, rs], adds=[dst])

        eps_t = {}
        st["off"] = st["gbase"]
        for nfeat in (D, SSD_W, 128, 1):
            t_ = alloc([1], F32, "eps%d" % nfeat)
            V(lambda e, t_=t_, nfeat=nfeat: e.memset(t_.ap, float(nfeat) * EPS), writes=[t_])
            eps_t[nfeat] = t_
        st["gbase"] = st["off"]

        def make_norm(ps_ssq, sqr, rs, lnv):
            stt_ = {"fed": [], "n": 0}

            def feed(src_ap, src_tl):
                i = stt_["n"]
                stt_["n"] += 1
                sq = sqr[i % len(sqr)]
                A(lambda e: e.activation(out=sq.ap, in_=src_ap, func=AF.Square), reads=[src_tl], writes=[sq])
                stt_["fed"].append((i, sq))

            def pump(flush=False):
                while stt_["fed"] and (flush or len(stt_["fed"]) > 1):
                    i, sq = stt_["fed"].pop(0)
                    MM(ps_ssq.ap, ones_b.ap, sq.ap, i == 0, i == NDC - 1, [ones_b, sq], ps_ssq, True)

            def finish(gc, src, dst_fn):
                pump(flush=True)
                assert stt_["n"] == NDC
                stt_["n"] = 0
                A(lambda e: e.activation(out=lnv.ap, in_=ps_ssq.ap, func=AF.Ln, bias=eps_t[D][:, 0:1], scale=1.0),
                  reads=[ps_ssq, eps_t[D]], writes=[lnv])
                A(lambda e: e.activation(out=rs.ap, in_=lnv.ap, func=AF.Exp, scale=-0.5), reads=[lnv], writes=[rs])
                for dc in range(NDC):
                    dap, dtl = dst_fn(dc)
                    V(lambda e, dc=dc, dap=dap: e.scalar_tensor_tensor(out=dap, in0=src[:, dc, :], scalar=gc[:, dc:dc + 1],
                                                                       in1=rs.ap, op0=ALU.mult, op1=ALU.mult),
                      reads=[src, gc, rs], writes=[dtl])

            return feed, pump, finish

        def phase_p1(l):
            reset_arena()
            order = []
            for (sname, S, base) in seqs:
                for ti in range(S // T):
                    order += [(l, "in", bi) for bi in range(len(W_IN_BLOCKS))]
            ring = WRing(4, order, lag=1)
            hT = alloc([NDC, T], F32, "hT")
            sq_off = st["off"]
            sq = alloc([NDC, T], BF16, "sq")
            uT = alloc([NDC, T], BF16, "uT")
            uTk = [Tl(uT[:, kc, :], Buf("uT%d" % kc)) for kc in range(NDC)]
            sqr = [alloc([T], BF16, "sqr%d" % i) for i in range(3)]
            rs = alloc([T], F32, "rs")
            lnv = alloc([T], F32, "lnv")
            sig = alloc([T], F32, "sig")
            stg_glu = alloc([4, T], BF16, "stg_glu")
            stg_z = alloc([4, 768], BF16, "stg_z")
            stg_x = alloc([10, T], BF16, "stg_x")
            stg_dt = alloc([4, 24], F32, "stg_dt")
            tdt = alloc([4, 24], F32, "tdt")
            stg_q = alloc([6, T], BF16, "stg_q")
            stg_k = alloc([6, T], BF16, "stg_k")
            stg_v = alloc([4, 768], BF16, "stg_v")
            ps_ssq = pbank(0)
            banks = [pbank(i) for i in range(1, 8)]
            bk = {"i": 0}

            def nb():
                b = banks[bk["i"] % len(banks)]
                bk["i"] += 1
                return b

            feed, pump, finish = make_norm(ps_ssq, sqr, rs, lnv)

            def fm_chunk(w, c, ps):
                for kc in range(NDC):
                    MM(ps.ap, w[:, kc, c * 128:(c + 1) * 128], uTk[kc].ap, kc == 0, kc == NDC - 1, [w, uTk[kc]], ps,
                       kc == NDC - 1)

            def tm_sub(w, bw, sub, ps):
                for kc in range(NDC):
                    MM(ps[:, 0:bw], uTk[kc][:, sub * 128:(sub + 1) * 128], w[:, kc, 0:bw], kc == 0, kc == NDC - 1,
                       [w, uTk[kc]], ps, kc == NDC - 1)

            tiles = [(sname, S, base, ti) for (sname, S, base) in seqs for ti in range(S // T)]

            def load_h(tl_):
                sname_, S_, base_, ti_ = tl_
                t0_ = base_ + ti_ * T
                LD(hT, hT.ap, hTd[:, :, t0_:t0_ + T].rearrange("c p s -> p c s"), reads=[DB("hT", sname_, ti_)])

            if l > 0:
                load_h(tiles[0])

            for it, (sname, S, base, ti) in enumerate(tiles):
                if True:
                    t0 = base + ti * T
                    if l == 0:
                        xt_ap = arena[:, sq_off:sq_off + 4 * D].rearrange("p (s d) -> p s d", s=4)
                        R.dma("sync", lambda e, sname=sname, ti=ti, xt_ap=xt_ap: e.dma_start(
                            out=xt_ap, in_=x_in[sname][ti * T:(ti + 1) * T, :].rearrange("(s p) d -> p s d", p=128)),
                            sq.buf, (), [sq.buf] + [t.buf for t in uTk], ())
                        for dc in range(NDC):
                            ps = nb()
                            for sub in range(4):
                                TR(ps[:, sub * 128:(sub + 1) * 128], xt_ap[:, sub, dc * 128:(dc + 1) * 128], ident_f.ap,
                                   [sq, ident_f] + uTk, ps, sub == 3)
                            copy_out(hT[:, dc, :], ps.ap, [ps], adds=[hT])
                        ST(hT, hTd[:, :, t0:t0 + T].rearrange("c p s -> p c s"), hT.ap, DB("hT", sname, ti), mode="w")
                    for dc in range(NDC):
                        feed(hT[:, dc, :], hT)
                        pump()
                    finish(gcol[(l, "mix")], hT, lambda dc: (uTk[dc].ap, uTk[dc]))
                    if l > 0 and it + 1 < len(tiles):
                        load_h(tiles[it + 1])
                    wv_, _, _ = ring.get()
                    wg_, _, _ = ring.get()
                    pvs = [nb() for _ in range(4)]
                    for kc in range(NDC):
                        for c in range(4):
                            MM(pvs[c].ap, wv_[:, kc, c * 128:(c + 1) * 128], uTk[kc].ap, kc == 0, kc == NDC - 1,
                               [wv_, uTk[kc]], pvs[c], kc == NDC - 1)
                    for c in range(4):
                        pv = pvs[c]
                        pg = nb()
                        fm_chunk(wg_, c, pg)
                        A(lambda e, pg=pg: e.activation(out=sig.ap, in_=pg.ap, func=AF.Sigmoid), reads=[pg], writes=[sig])
                        V(lambda e, pv=pv, c=c: e.tensor_tensor(out=stg_glu[:, c, :], in0=pv.ap, in1=sig.ap, op=ALU.mult),
                          reads=[pv, sig], adds=[stg_glu])
                    ST(stg_glu, gluT[:, :, t0:t0 + T].rearrange("c p s -> p c s"), stg_glu.ap, DB("glu", sname, ti), mode="w")
                    for (zc0, zbw) in ((0, 512), (512, 256)):
                        w, _, bw = ring.get()
                        for sub in range(4):
                            ps = nb()
                            tm_sub(w, bw, sub, ps)
                            A(lambda e, ps=ps, sub=sub, zc0=zc0, bw=bw: e.activation(
                                out=stg_z[:, sub, zc0:zc0 + bw], in_=ps[:, 0:bw], func=AF.Silu), reads=[ps], adds=[stg_z])
                    ST(stg_z, zsd[t0:t0 + T, :].rearrange("(s p) c -> p s c", p=128), stg_z.ap, DB("zs", sname, ti), mode="w")
                    cc = 0
                    for nchunk in (4, 4, 2):
                        w, _, bw = ring.get()
                        for c in range(nchunk):
                            ps = nb()
                            fm_chunk(w, c, ps)
                            copy_out(stg_x[:, cc, :], ps.ap, [ps], adds=[stg_x])
                            cc += 1
                    ST(stg_x, xbcT[:, :, t0:t0 + T].rearrange("c p s -> p c s"), stg_x.ap, DB("xbc", sname, ti), mode="w")
                    w, _, bw = ring.get()
                    for sub in range(4):
                        ps = nb()
                        tm_sub(w, 24, sub, ps)
                        V(lambda e, ps=ps, sub=sub: e.tensor_tensor(out=tdt[:, sub, :], in0=ps[:, 0:24], in1=dtb_b[l].ap,
                                                                    op=ALU.add), reads=[ps, dtb_b[l]], adds=[tdt])
                    A(lambda e: e.activation(out=tdt.ap, in_=tdt.ap, func=AF.Exp), reads=[tdt], writes=[tdt])
                    A(lambda e: e.activation(out=stg_dt.ap, in_=tdt.ap, func=AF.Ln, bias=1.0), reads=[tdt], writes=[stg_dt])
                    ST(stg_dt, dtd[t0:t0 + T, :].rearrange("(s p) c -> p s c", p=128), stg_dt.ap, DB("dt", sname, ti), mode="w")
                    for (stg, dst, sc, nm) in ((stg_q, qTd, 0.125, "q"), (stg_k, kTd, None, "k")):
                        cc = 0
                        for nchunk in (4, 2):
                            w, _, bw = ring.get()
                            for c in range(nchunk):
                                ps = nb()
                                fm_chunk(w, c, ps)
                                copy_out(stg[:, cc, :], ps.ap, [ps], adds=[stg], scale=sc)
                                cc += 1
                        ST(stg, dst[:, :, t0:t0 + T].rearrange("c p s -> p c s"), stg.ap, DB(nm, sname, ti), mode="w")
                    for (vc0, vbw) in ((0, 512), (512, 256)):
                        w, _, bw = ring.get()
                        for sub in range(4):
                            ps = nb()
                            tm_sub(w, bw, sub, ps)
                            copy_out(stg_v[:, sub, vc0:vc0 + bw], ps[:, 0:bw], [ps], adds=[stg_v])
                    ST(stg_v, vd[t0:t0 + T, :].rearrange("(s p) c -> p s c", p=128), stg_v.ap, DB("v", sname, ti), mode="w")
            R.barrier()

        def phase_conv(l):
            reset_arena()
            dg = alloc([4, CONV_W, 128], BF16, "dg")
            for c in range(4):
                for j in range(CONV_W):
                    V(lambda e, c=c, j=j: e.tensor_scalar_mul(out=dg[:, c, j, :], in0=ident_f.ap,
                                                              scalar1=cwT[l][:, c, j:j + 1]),
                      reads=[ident_f, cwT[l]], adds=[dg])
            import os
            CUT = int(os.environ.get("CUT", "9"))
            if CUT <= 1:
                R.barrier()
                return
            HW = CONV_W // 2
            gins = [alloc([4, T + 2 * HW], BF16, "gin%d" % i) for i in range(2)]
            xc = alloc([4, T], F32, "xc")
            xcb = alloc([4, T], BF16, "xcb")
            sqb = alloc([4, T], BF16, "sqb")
            msq = alloc([T], F32, "msq")
            var = alloc([T], F32, "var")
            rstd = alloc([T], F32, "rstd")
            t1 = alloc([T], F32, "t1")
            ystg = alloc([4, T], BF16, "ystg")
            pcs = [pbank(i) for i in range(0, 4)]
            pmean = pbank(4)
            pex2 = pbank(5)
            it = 0
            for (sname, S, base) in seqs:
                nt = S // T
                for ti in range(nt):
                    t0 = base + ti * T
                    gin = gins[it % 2]
                    it += 1
                    lo = max(0, ti * T - HW)
                    hi = min(S, (ti + 1) * T + HW)
                    if ti == 0 or ti == nt - 1:
                        V(lambda e, gin=gin: e.memset(gin.ap, 0.0), writes=[gin])
                    o0 = lo - (ti * T - HW)
                    LD(gin, gin[:, :, o0:o0 + (hi - lo)], gluT[:, :, base + lo:base + hi].rearrange("c p s -> p c s"),
                       reads=[DB("glu", sname, tj) for tj in range(max(0, ti - 1), min(nt, ti + 2))])
                    for c in range(4):
                        ps = pcs[c]
                        for j in range(CONV_W):
                            MM(ps.ap, dg[:, c, j, :], gin[:, c, j:j + T], j == 0, j == CONV_W - 1, [dg, gin], ps,
                               j == CONV_W - 1)
                        A(lambda e, ps=ps, c=c: e.activation(out=xc[:, c, :], in_=ps.ap, func=AF.Identity,
                                                             bias=cb_col[l][:, c:c + 1], scale=1.0),
                          reads=[ps, cb_col[l]], adds=[xc])
                        A(lambda e, ps=ps, c=c: e.activation(out=sqb[:, c, :], in_=ps.ap, func=AF.Square,
                                                             bias=cb_col[l][:, c:c + 1], scale=1.0),
                          reads=[ps, cb_col[l]], adds=[sqb])
                    if CUT <= 2:
                        continue
                    V(lambda e: e.tensor_copy(out=xcb.ap, in_=xc.ap), reads=[xc], writes=[xcb])
                    for c in range(4):
                        MM(pmean.ap, ones512_b.ap, xcb[:, c, :], c == 0, c == 3, [ones512_b, xcb], pmean, c == 3)
                    for c in range(4):
                        MM(pex2.ap, ones512_b.ap, sqb[:, c, :], c == 0, c == 3, [ones512_b, sqb], pex2, c == 3)
                    A(lambda e: e.activation(out=msq.ap, in_=pmean.ap, func=AF.Square), reads=[pmean], writes=[msq])
                    V(lambda e: e.tensor_tensor(out=var.ap, in0=pex2.ap, in1=msq.ap, op=ALU.subtract), reads=[pex2, msq],
                      writes=[var])
                    V(lambda e: e.tensor_scalar_max(out=var.ap, in0=var.ap, scalar1=0.0), reads=[var], writes=[var])
                    A(lambda e: e.activation(out=var.ap, in_=var.ap, func=AF.Sqrt, bias=eps_t[1][:, 0:1], scale=1.0),
                      reads=[var, eps_t[1]], writes=[var])
                    V(lambda e: e.reciprocal(out=rstd.ap, in_=var.ap), reads=[var], writes=[rstd])
                    if CUT <= 3:
                        continue
                    for c in range(4):
                        V(lambda e, c=c: e.tensor_tensor(out=t1.ap, in0=xc[:, c, :], in1=pmean.ap, op=ALU.subtract),
                          reads=[xc, pmean], writes=[t1])
                        V(lambda e: e.tensor_tensor(out=t1.ap, in0=t1.ap, in1=rstd.ap, op=ALU.mult), reads=[t1, rstd],
                          writes=[t1])
                        A(lambda e, c=c: e.activation(out=ystg[:, c, :], in_=t1.ap, func=AF.Silu,
                                                      bias=cnb_col[l][:, c:c + 1], scale=cng_col[l][:, c:c + 1]),
                          reads=[t1, cnb_col[l], cng_col[l]], adds=[ystg])
                    ST(ystg, mixT[0:4, :, t0:t0 + T].rearrange("c p s -> p c s"), ystg.ap, DB("mixc", sname, ti), mode="w")
            R.barrier()

        def phase_ssd(l):
            reset_arena()
            if l + 1 < nlayers:
                convert_weights(l + 1)
            dgs = alloc([10, SSD_CW, 128], BF16, "dgs")
            for c in range(10):
                for j in range(SSD_CW):
                    V(lambda e, c=c, j=j: e.tensor_scalar_mul(out=dgs[:, c, j, :], in0=ident_f.ap,
                                                              scalar1=scwT[l][:, c, j:j + 1]),
                      reads=[ident_f, scwT[l]], adds=[dgs])
            xins = [alloc([10, T + 4], BF16, "xin%d" % i) for i in range(2)]
            xs4s = [alloc([4, 768], F32, "xs4_%d" % i) for i in range(2)]
            bt4s = [alloc([4, 256], BF16, "bt4_%d" % i) for i in range(2)]
            bcts = [alloc([4, T], BF16, "bct%d" % i) for i in range(2)]
            acss = [alloc([SSD_H], F32, "acs%d" % i) for i in range(2)]
            UA = alloc([SSD_H, 128], F32, "UA")
            expRs = [alloc([SSD_H, 128], F32, "expR%d" % i) for i in range(2)]
            Dms = [alloc([SSD_H, 128], F32, "Dm%d" % i) for i in range(2)]
            CBm = alloc([2, 128], F32, "CBm")
            MTs = [alloc([SSD_H, 128], BF16, "MT%d" % i) for i in range(2)]
            CdTs = [alloc([SSD_H, 128], BF16, "CdT%d" % i) for i in range(2)]
            xdts = [alloc([768], BF16, "xdt%d" % i) for i in range(2)]
            xdecs = [alloc([768], BF16, "xdec%d" % i) for i in range(2)]
            dtdec = alloc([SSD_H], F32, "dtdec")
            prev = alloc([768], F32, "prev")
            prev_bs = [alloc([768], BF16, "prevb%d" % i) for i in range(2)]
            tmp = alloc([768], F32, "tmp")
            tmp2 = alloc([768], F32, "tmp2")
            yfo = [alloc([768], F32, "yfo%d" % i) for i in range(2)]
            yfi = [alloc([768], F32, "yfi%d" % i) for i in range(2)]
            zsi = [alloc([768], BF16, "zsi%d" % i) for i in range(2)]
            ysum = alloc([768], F32, "ysum")
            junk = alloc([768], BF16, "junkb")
            ssq = alloc([1], F32, "ssq")
            lnv = alloc([1], F32, "lnv")
            rr = alloc([1], F32, "rr")
            ob = alloc([768], BF16, "ob")
            ystgs = [alloc([6, T], BF16, "ystg%d" % i) for i in range(2)]
            psX = Tl(psum[:, 0:2, :].rearrange("p a b -> p (a b)"), Buf("psX"))
            psF = pbank(2)
            psFb = Tl(psum[:, 2, :].bitcast(BF16), psF.buf)
            pRs = [pbank(3), pbank(4)]
            p5 = pbank(5)
            pY = Tl(psum[:, 6:8, :].rearrange("p a b -> p (a b)"), Buf("pY"))
            pacs = Tl(psum[:, 7, 400:412], Buf("pacs"))
            ctr = {"x": 0, "r": 0, "m": 0, "y": 0, "pb": 0, "fi": 0, "st": 0, "blk": 0}

            for (sname, S, base) in seqs:
                nch = S // 128
                nblk = S // T
                dt_all = alloc([nch, 24], F32, "dt_all_" + sname)
                a_all = alloc([nch, 24], F32, "a_all_" + sname)
                LD(dt_all, dt_all.ap, dtd[base:base + S, :].rearrange("(c p) k -> p c k", p=128),
                   reads=[DB("dt", sname, tj) for tj in range(nblk)])
                V(lambda e, a_all=a_all, dt_all=dt_all, nch=nch: e.tensor_tensor(
                    out=a_all.ap, in0=dt_all.ap, in1=A_b[l].ap.unsqueeze(1).broadcast_to([128, nch, 24]), op=ALU.mult),
                  reads=[dt_all, A_b[l]], writes=[a_all])
                for d in range(2):
                    Tri = U_f if d == 0 else L_f
                    lend = 127 if d == 0 else 0
                    V(lambda e: e.memset(prev.ap, 0.0), writes=[prev])
                    st_ = {"pb": prev_bs[ctr["pb"] % 2]}
                    ctr["pb"] += 1
                    V(lambda e, t_=st_["pb"]: e.memset(t_.ap, 0.0), writes=[st_["pb"]])
                    blk_order = list(range(nblk)) if d == 0 else list(range(nblk - 1, -1, -1))
                    ch_order = list(range(4)) if d == 0 else list(range(3, -1, -1))
                    chunks = [(bi, ch) for bi in blk_order for ch in ch_order]

                    def stageA(c, par, Tri=Tri, a_all=a_all, d=d):
                        a_v = a_all[:, c, d * 12:d * 12 + 12]
                        acs = acss[par]
                        expR = expRs[par]
                        Dm = Dms[par]
                        MM(pacs.ap, Tri.ap, a_v, True, True, [Tri, a_all], pacs, True)
                        V(lambda e: e.tensor_scalar_mul(out=acs.ap, in0=pacs.ap, scalar1=-1.0), reads=[pacs], writes=[acs])
                        V(lambda e: e.tensor_tensor(
                            out=UA.ap, in0=Tri.ap.unsqueeze(1).broadcast_to([128, SSD_H, 128]),
                            in1=a_v.unsqueeze(2).broadcast_to([128, SSD_H, 128]), op=ALU.mult),
                          reads=[Tri, a_all], writes=[UA])
                        for pc in range(3):
                            pR = pRs[ctr["r"] % 2]
                            ctr["r"] += 1
                            MM(pR.ap, ones_f.ap, UA[:, pc * 4:(pc + 1) * 4, :], True, True, [ones_f, UA], pR, True)
                            A(lambda e, pR=pR, pc=pc: e.activation(
                                out=expR[:, pc * 4:(pc + 1) * 4, :], in_=pR.ap.rearrange("p (h l) -> p h l", h=4),
                                func=AF.Exp), reads=[pR], adds=[expR])
                            for hh in range(4):
                                h = pc * 4 + hh
                                A(lambda e, pR=pR, hh=hh, h=h: e.activation(
                                    out=Dm[:, h, :], in_=pR[:, hh * 128:(hh + 1) * 128], func=AF.Exp,
                                    bias=acs[:, h:h + 1], scale=1.0), reads=[pR, acs], adds=[Dm])

                    xin_ready = {}

                    def load_xin(bi, sname=sname, S=S, base=base, nblk=nblk):
                        xin = xins[ctr["x"] % 2]
                        ctr["x"] += 1
                        lo = max(0, bi * T - 2)
                        hi = min(S, (bi + 1) * T + 2)
                        if bi == 0 or bi == nblk - 1:
                            V(lambda e, xin=xin: e.memset(xin.ap, 0.0), writes=[xin])
                        o0 = lo - (bi * T - 2)
                        LD(xin, xin[:, :, o0:o0 + (hi - lo)], xbcT[:, :, base + lo:base + hi].rearrange("c p s -> p c s"),
                           reads=[DB("xbc", sname, tj) for tj in range(max(0, bi - 1), min(nblk, bi + 2))])
                        return xin

                    def block_conv(bi, sname=sname, S=S, base=base, nblk=nblk):
                        if bi in xin_ready:
                            xin = xin_ready.pop(bi)
                        else:
                            xin = load_xin(bi)
                        k = ctr["blk"] % 2
                        ctr["blk"] += 1
                        xs4, bt4, bct = xs4s[k], bt4s[k], bcts[k]
                        for ch in range(4):
                            for cc in range(8):
                                first = (cc % 4 == 0)
                                for j in range(SSD_CW):
                                    MM(psX[:, cc * 128:(cc + 1) * 128], xin[:, cc, ch * 128 + j:ch * 128 + j + 128],
                                       dgs[:, cc, j, :], first and j == 0, False, [xin, dgs], psX, False,
                                       skip_group_check=True)
                                MM(psX[:, cc * 128:(cc + 1) * 128], ones_b[0:1, :], scb_row[l][0:1, cc * 128:(cc + 1) * 128],
                                   False, True, [ones_b, scb_row[l]], psX, cc == 7, skip_group_check=True)
                            A(lambda e, ch=ch: e.activation(out=xs4[:, ch, :], in_=psX[:, 0:768], func=AF.Silu),
                              reads=[psX], adds=[xs4])
                            A(lambda e, ch=ch: e.activation(out=bt4[:, ch, :], in_=psX[:, 768:1024], func=AF.Silu),
                              reads=[psX], adds=[bt4])
                        for k4 in range(4):
                            cc = 6 + k4
                            for j in range(SSD_CW):
                                MM(psF.ap, dgs[:, cc, j, :], xin[:, cc, j:j + T], j == 0, j == SSD_CW - 1, [dgs, xin], psF,
                                   j == SSD_CW - 1)
                            A(lambda e, k4=k4, cc=cc: e.activation(out=bct[:, k4, :], in_=psF.ap, func=AF.Silu,
                                                                   bias=scb_col[l][:, cc:cc + 1], scale=1.0),
                              reads=[psF, scb_col[l]], adds=[bct])
                        return xs4, bt4, bct

                    def stageB(bi, ch, par, blkbufs, ystg, sname=sname, base=base, d=d, lend=lend, Tri=Tri,
                               dt_all=dt_all):
                        xs4, bt4, bct = blkbufs
                        c = bi * 4 + ch
                        tc0 = base + bi * T + ch * 128
                        dt_v = dt_all[:, c, d * 12:d * 12 + 12]
                        expR = expRs[par]
                        Dm = Dms[par]
                        if d == 1:
                            yin = yfi[ctr["fi"] % 2]
                            zin = zsi[ctr["fi"] % 2]
                            ctr["fi"] += 1
                            LD(yin, yin.ap, yfd[tc0:tc0 + 128, :], reads=[DB("yf", sname, c)])
                            LD(zin, zin.ap, zsd[tc0:tc0 + 128, :], reads=[DB("zs", sname, bi)])
                        for g in range(2):
                            MM(p5[:, g * 128:(g + 1) * 128], bct[:, g, ch * 128:(ch + 1) * 128],
                               bct[:, 2 + g, ch * 128:(ch + 1) * 128], True, True, [bct], p5, g == 1)
                        V(lambda e: e.tensor_tensor(
                            out=CBm.ap, in0=p5[:, 0:256].rearrange("p (g l) -> p g l", g=2),
                            in1=Tri.ap.unsqueeze(1).broadcast_to([128, 2, 128]), op=ALU.mult),
                          reads=[p5, Tri], writes=[CBm])
                        MT = MTs[ctr["m"] % 2]
                        CdT = CdTs[ctr["m"] % 2]
                        xdt = xdts[ctr["m"] % 2]
                        xdec = xdecs[ctr["m"] % 2]
                        ctr["m"] += 1
                        for g in range(2):
                            V(lambda e, g=g: e.scalar_tensor_tensor(
                                out=MT[:, g * 6:(g + 1) * 6, :], in0=Dm[:, g * 6:(g + 1) * 6, :], scalar=1.0,
                                in1=CBm[:, g:g + 1, :].broadcast_to([128, 6, 128]), op0=ALU.min, op1=ALU.mult),
                              reads=[Dm, CBm], adds=[MT])
                        V(lambda e: e.tensor_tensor(
                            out=CdT.ap.rearrange("p (g r) l -> p g r l", g=2),
                            in0=expR.ap.rearrange("p (g r) l -> p g r l", g=2),
                            in1=bct[:, 2:4, ch * 128:(ch + 1) * 128].unsqueeze(2).broadcast_to([128, 2, 6, 128]),
                            op=ALU.mult), reads=[expR, bct], writes=[CdT])
                        V(lambda e: e.tensor_tensor(out=dtdec.ap, in0=dt_v, in1=Dm[:, :, lend], op=ALU.mult),
                          reads=[dt_all, Dm], writes=[dtdec])
                        xs_v = xs4[:, ch, :].rearrange("p (h q) -> p h q", h=SSD_H)
                        V(lambda e: e.tensor_tensor(
                            out=xdt.ap.rearrange("p (h q) -> p h q", h=SSD_H), in0=xs_v,
                            in1=dt_v.unsqueeze(2).broadcast_to([128, SSD_H, 64]), op=ALU.mult),
                          reads=[xs4, dt_all], writes=[xdt])
                        V(lambda e: e.tensor_tensor(
                            out=xdec.ap.rearrange("p (h q) -> p h q", h=SSD_H), in0=xs_v,
                            in1=dtdec.ap.unsqueeze(2).broadcast_to([128, SSD_H, 64]), op=ALU.mult),
                          reads=[xs4, dtdec], writes=[xdec])
                        pbcur = st_["pb"]
                        for h in range(SSD_H):
                            MM(pY[:, h * 64:(h + 1) * 64], MT[:, h, :], xdt[:, h * 64:(h + 1) * 64], h in (0, 8), False,
                               [MT, xdt], pY, False, skip_group_check=True)
                            MM(pY[:, h * 64:(h + 1) * 64], CdT[:, h, :], pbcur[:, h * 64:(h + 1) * 64], False, True,
                               [CdT, pbcur], pY, h == SSD_H - 1, skip_group_check=True)
                        for h in range(SSD_H):
                            g = h // 6
                            MM(psX[:, h * 64:(h + 1) * 64], bt4[:, ch, g * 128:(g + 1) * 128], xdec[:, h * 64:(h + 1) * 64],
                               h in (0, 8), True, [bt4, xdec], psX, h == SSD_H - 1, skip_group_check=True)
                        V(lambda e: e.tensor_tensor(
                            out=tmp.ap.rearrange("p (h q) -> p h q", h=SSD_H),
                            in0=prev.ap.rearrange("p (h q) -> p h q", h=SSD_H),
                            in1=expR[:, :, lend].unsqueeze(2).broadcast_to([128, SSD_H, 64]), op=ALU.mult),
                          reads=[prev, expR], writes=[tmp])
                        V(lambda e: e.tensor_tensor(out=prev.ap, in0=tmp.ap, in1=psX[:, 0:768], op=ALU.add),
                          reads=[tmp, psX], writes=[prev])
                        st_["pb"] = prev_bs[ctr["pb"] % 2]
                        ctr["pb"] += 1
                        A(lambda e, t_=st_["pb"]: e.activation(out=t_.ap, in_=prev.ap, func=AF.Copy), reads=[prev],
                          writes=[st_["pb"]])
                        if d == 0:
                            yo = yfo[ctr["y"] % 2]
                            ctr["y"] += 1
                            A(lambda e: e.activation(out=yo.ap, in_=pY[:, 0:768], func=AF.Copy), reads=[pY], writes=[yo])
                            ST(yo, yfd[tc0:tc0 + 128, :], yo.ap, DB("yf", sname, c), mode="w")
                        else:
                            V(lambda e: e.tensor_tensor(out=ysum.ap, in0=pY[:, 0:768], in1=yin.ap, op=ALU.add),
                              reads=[pY, yin], writes=[ysum])
                            V(lambda e: e.tensor_tensor(
                                out=tmp2.ap.rearrange("p (h q) -> p h q", h=SSD_H), in0=xs_v,
                                in1=D_b[l].ap.unsqueeze(2).broadcast_to([128, SSD_H, 64]), op=ALU.mult),
                              reads=[xs4, D_b[l]], writes=[tmp2])
                            V(lambda e: e.tensor_tensor(out=ysum.ap, in0=ysum.ap, in1=tmp2.ap, op=ALU.add),
                              reads=[ysum, tmp2], writes=[ysum])
                            V(lambda e: e.tensor_tensor(out=ysum.ap, in0=ysum.ap, in1=zin.ap, op=ALU.mult),
                              reads=[ysum, zin], writes=[ysum])
                            V(lambda e: e.scalar_tensor_tensor(out=junk.ap, in0=ysum.ap, scalar=1.0, in1=ysum.ap,
                                                               op0=ALU.mult, op1=ALU.mult, accum_out=ssq.ap),
                              reads=[ysum], writes=[junk, ssq])
                            A(lambda e: e.activation(out=lnv.ap, in_=ssq.ap, func=AF.Ln, bias=eps_t[SSD_W][:, 0:1], scale=1.0),
                              reads=[ssq, eps_t[SSD_W]], writes=[lnv])
                            A(lambda e: e.activation(out=rr.ap, in_=lnv.ap, func=AF.Exp, scale=-0.5), reads=[lnv], writes=[rr])
                            V(lambda e: e.scalar_tensor_tensor(out=ob.ap, in0=ysum.ap, scalar=rr[:, 0:1], in1=gss_b[l].ap,
                                                               op0=ALU.mult, op1=ALU.mult),
                              reads=[ysum, rr, gss_b[l]], writes=[ob])
                            for i6 in range(6):
                                TR(psFb[:, i6 * 128:(i6 + 1) * 128], ob[:, i6 * 128:(i6 + 1) * 128], ident_b.ap,
                                   [ob, ident_b], psF, i6 == 5)
                            A(lambda e: e.activation(
                                out=ystg[:, :, ch * 128:(ch + 1) * 128],
                                in_=psFb[:, 0:768].rearrange("p (c s) -> p c s", c=6), func=AF.Copy),
                              reads=[psF], adds=[ystg])

                    blkbufs = None
                    ystg = None
                    stageA(chunks[0][0] * 4 + chunks[0][1], 0)
                    for i, (bi, ch) in enumerate(chunks):
                        if i % 4 == 0:
                            blkbufs = block_conv(bi)
                            if i + 4 < len(chunks):
                                nbi = chunks[i + 4][0]
                                xin_ready[nbi] = load_xin(nbi)
                            ystg = ystgs[ctr["st"] % 2]
                            if d == 1:
                                ctr["st"] += 1
                        if i + 1 < len(chunks):
                            nb_, nc_ = chunks[i + 1]
                            stageA(nb_ * 4 + nc_, (i + 1) % 2)
                        stageB(bi, ch, i % 2, blkbufs, ystg)
                        if d == 1 and i % 4 == 3:
                            t0 = base + bi * T
                            ST(ystg, mixT[4:10, :, t0:t0 + T].rearrange("c p s -> p c s"), ystg.ap, DB("mixs", sname, bi),
                               mode="w")
            R.barrier()

        def phase_attn(l):
            reset_arena()
            Hb = alloc([ATT_H, 1152], BF16, "Hb")
            hstage = alloc([1152], F32, "hstage")
            for h in range(ATT_H):
                LD(hstage, hstage.ap, bass.AP(wvd.tensor, h * NBIAS, [[1, 128], [1, 1152]]), reads=[DB("wv")])
                V(lambda e, h=h: e.tensor_copy(out=Hb[:, h, :], in_=hstage.ap), reads=[hstage], adds=[Hb])
            SMAX = max(s[1] for s in seqs)
            qts = [alloc([SMAX], BF16, "qt%d" % i) for i in range(2)]
            kts = [alloc([SMAX], BF16, "kt%d" % i) for i in range(2)]
            vas = [alloc([SMAX // 128, 128], BF16, "va%d" % i) for i in range(2)]
            pTs = [alloc([2, T], BF16, "pT%d" % i) for i in range(3)]
            r1 = alloc([T], F32, "r1")
            t1 = alloc([T], F32, "t1a")
            r2 = alloc([T], F32, "r2")
            t2 = alloc([T], F32, "t2a")
            o_t = alloc([T], F32, "o_t")
            sqo = alloc([T], BF16, "sqo")
            sdv = alloc([T], F32, "sdv")
            ystgs = [alloc([T], BF16, "ystga%d" % i) for i in range(2)]
            zacc = [[alloc([T], F32, "zacc%d_%d" % (i, k)) for k in range(2)] for i in range(2)]
            import os
            ZENG = os.environ.get("ZENG", "vector,gpsimd").split(",")
            pSs = [Tl(psum[:, 0:2, :], Buf("pS0")), Tl(psum[:, 2:4, :], Buf("pS1"))]
            pN = [pbank(4), pbank(5)]
            pZ = [pbank(6), pbank(7)]
            ctr = {"hd": 0, "y": 0}
            heads = [(sname, S, base, h) for (sname, S, base) in seqs for h in range(ATT_H)]

            def load_head(idx):
                sname_, S_, base_, h_ = heads[idx]
                nt_ = S_ // T
                qt_, kt_, va_ = qts[idx % 2], kts[idx % 2], vas[idx % 2]
                LD(qt_, qt_[:, 0:S_], qTd[h_, :, base_:base_ + S_], reads=[DB("q", sname_, tj) for tj in range(nt_)])
                LD(kt_, kt_[:, 0:S_], kTd[h_, :, base_:base_ + S_], reads=[DB("k", sname_, tj) for tj in range(nt_)])
                LD(va_, va_[:, 0:S_ // 128, :],
                   vd[base_:base_ + S_, h_ * 128:(h_ + 1) * 128].rearrange("(j p) e -> p j e", p=128),
                   reads=[DB("v", sname_, tj) for tj in range(nt_)])
                return qt_, kt_, va_

            loaded = {0: load_head(0)}
            for hidx, (sname, S, base, h) in enumerate(heads):
                nk = S // 128
                nq = S // T
                ntile = S // T
                if True:
                    qt, kt, va = loaded.pop(hidx)
                    if hidx + 1 < len(heads):
                        loaded[hidx + 1] = load_head(hidx + 1)

                    def emit_S(Q, j, kt=kt, qt=qt, h=h):
                        step = Q * nk + j
                        pS = pSs[step % 2]
                        pT = pTs[step % 3]
                        dlt = 128 * j - 512 * Q
                        near = -128 <= dlt <= 512
                        for m in range(2):
                            MM(pS[:, m, :], kt[64 * m:64 * m + 64, j * 128:(j + 1) * 128],
                               qt[64 * m:64 * m + 64, Q * T:(Q + 1) * T], True, not near, [kt, qt], pS,
                               (m == 1) and not near)
                        if near:
                            off = 512 - dlt
                            for m in range(2):
                                MM(pS[:, m, :], J_b.ap, Hb[:, h, off:off + T], False, True, [J_b, Hb], pS, m == 1)
                            A(lambda e, pT=pT, pS=pS: e.activation(out=pT.ap, in_=pS.ap, func=AF.Exp), reads=[pS],
                              writes=[pT])
                        else:
                            side = 0 if dlt < 0 else 1
                            A(lambda e, pT=pT, pS=pS, side=side, h=h: e.activation(
                                out=pT.ap, in_=pS.ap, func=AF.Exp, bias=cfar[:, side, h:h + 1], scale=1.0),
                              reads=[pS, cfar], writes=[pT])

                    def emit_AV(Q, j, va=va):
                        step = Q * nk + j
                        pT = pTs[step % 3]
                        for m in range(2):
                            MM(pN[m].ap, va[:, j, :], pT[:, m, :], j == 0, j == nk - 1, [va, pT], pN[m], j == nk - 1)
                        za = zacc[0][j % 2]
                        if j < 2:
                            V(lambda e, pT=pT, za=za: e.tensor_copy(out=za.ap, in_=pT[:, 0, :]), reads=[pT], writes=[za])
                        else:
                            V(lambda e, pT=pT, za=za: e.tensor_tensor(out=za.ap, in0=za.ap, in1=pT[:, 0, :], op=ALU.add),
                              reads=[pT, za], writes=[za])
                        MM(pZ[1].ap, ones_b.ap, pT[:, 1, :], j == 0, j == nk - 1, [ones_b, pT], pZ[1], j == nk - 1)
                        if j == nk - 1:
                            MM(pZ[0].ap, ones_f.ap, zacc[0][0].ap, True, False, [ones_f, zacc[0][0]], pZ[0], False)
                            MM(pZ[0].ap, ones_f.ap, zacc[0][1].ap, False, True, [ones_f, zacc[0][1]], pZ[0], True)

                    emit_S(0, 0)
                    fin_q = []
                    for Q in range(nq):
                        ystg = ystgs[ctr["y"] % 2]
                        ctr["y"] += 1
                        for j in range(nk):
                            if j + 1 < nk:
                                emit_S(Q, j + 1)
                            elif Q + 1 < nq:
                                emit_S(Q + 1, 0)
                            if j == nk - 1:
                                while fin_q:
                                    fin_q.pop(0)()
                            emit_AV(Q, j)
                            if fin_q:
                                fin_q.pop(0)()
                        V(lambda e: e.tensor_copy(out=t1.ap, in_=pN[0].ap), reads=[pN[0]], writes=[t1])
                        V(lambda e: e.tensor_copy(out=t2.ap, in_=pN[1].ap), reads=[pN[1]], writes=[t2])
                        V(lambda e: e.tensor_copy(out=r2.ap, in_=pZ[1].ap), reads=[pZ[1]], writes=[r2])
                        V(lambda e: e.tensor_copy(out=r1.ap, in_=pZ[0].ap), reads=[pZ[0]], writes=[r1])

                        def f1():
                            V(lambda e: e.reciprocal(out=r1.ap, in_=r1.ap), reads=[r1], writes=[r1])

                        def f2():
                            V(lambda e: e.tensor_tensor(out=t1.ap, in0=t1.ap, in1=r1.ap, op=ALU.mult), reads=[t1, r1], writes=[t1])

                        def f3():
                            V(lambda e: e.reciprocal(out=r2.ap, in_=r2.ap), reads=[r2], writes=[r2])

                        def f4():
                            V(lambda e: e.tensor_tensor(out=t2.ap, in0=t2.ap, in1=r2.ap, op=ALU.mult), reads=[t2, r2], writes=[t2])

                        def f5():
                            V(lambda e: e.scalar_tensor_tensor(out=o_t.ap, in0=t2.ap, scalar=neglam[l][:, 0:1], in1=t1.ap,
                                                               op0=ALU.mult, op1=ALU.add), reads=[t2, t1, neglam[l]], writes=[o_t])

                        def f6():
                            V(lambda e: e.tensor_tensor(out=sqo.ap, in0=o_t.ap, in1=o_t.ap, op=ALU.mult), reads=[o_t], writes=[sqo])

                        def f7():
                            MM(pZ[0].ap, ones_b.ap, sqo.ap, True, True, [ones_b, sqo], pZ[0], True)

                        def f8():
                            A(lambda e: e.activation(out=sdv.ap, in_=pZ[0].ap, func=AF.Ln, bias=eps_t[128][:, 0:1], scale=1.0),
                              reads=[pZ[0], eps_t[128]], writes=[sdv])

                        def f9():
                            A(lambda e: e.activation(out=sdv.ap, in_=sdv.ap, func=AF.Exp, scale=-0.5), reads=[sdv], writes=[sdv])

                        def f10(ystg=ystg, Q=Q, h=h, sname=sname, base=base):
                            V(lambda e: e.scalar_tensor_tensor(out=ystg.ap, in0=o_t.ap, scalar=gsub_col[l][:, 0:1],
                                                               in1=sdv.ap, op0=ALU.mult, op1=ALU.mult),
                              reads=[o_t, gsub_col[l], sdv], writes=[ystg])
                            ST(ystg, mixT[10 + h, :, base + Q * T:base + (Q + 1) * T], ystg.ap, DB("mixa", sname, Q, h), mode="w")

                        fin_q = [f1, f2, f3, f4, f5, f6, f7, f8, f9, f10]
                    while fin_q:
                        fin_q.pop(0)()
            R.barrier()

        def phase_p3(l):
            reset_arena()
            last = (l == nlayers - 1)
            tiles = [(sname, S, base, ti) for (sname, S, base) in seqs for ti in range(S // T)]
            order = []
            for _ in tiles:
                order += [(l, "out", bi) for bi in range(4)]
                order += [(l, "up", bi) for bi in range(16)]
                order += [(l, "down", bi) for bi in range(16)]
                order += [(l, "gate", bi) for bi in range(4)]
            ring = WRing(3, order)
            wple = alloc([2, D], BF16, "wple")
            off, kc, c0, bw = wspec[(l, "ple")][1][0]
            LD(wple, wple.ap, wdst(off, kc, bw), reads=[wbuf[(l, "ple", 0)]])
            hT = alloc([NDC, T], F32, "hT")
            aT = alloc([NDC, T], BF16, "aT")
            aTk = [Tl(aT[:, kc, :], Buf("aT%d" % kc)) for kc in range(NDC)]
            hid_off = st["off"]
            hid = alloc([64, T], BF16, "hid")
            ystg = Tl(arena[:, hid_off:hid_off + 4 * D].rearrange("p (s d) -> p s d", s=4), hid.buf)
            sqr = [alloc([T], BF16, "sqr%d" % i) for i in range(3)]
            rs = alloc([T], F32, "rs")
            lnv = alloc([T], F32, "lnv")
            rl = alloc([T], F32, "rl")
            tmp = alloc([T], F32, "tmp3")
            pf = alloc([4, PLE], F32, "pf")
            pb = alloc([4, PLE], BF16, "pb")
            pTt = alloc([2, T], BF16, "pTt")
            ps_ssq = pbank(0)
            ps_pt = pbank(1)
            ps_ptb = Tl(psum[:, 1, :].bitcast(BF16), ps_pt.buf)
            banks = [pbank(i) for i in range(2, 8)]
            bk = {"i": 0}
            feed, pump, finish = make_norm(ps_ssq, sqr, rs, lnv)

            def nb():
                b = banks[bk["i"] % len(banks)]
                bk["i"] += 1
                return b

            def load_tile_inputs(tl_, what):
                sname, S, base, ti = tl_
                t0 = base + ti * T
                if what == "mix":
                    for kc in range(NDC):
                        rd = [DB("mixc", sname, ti)] if kc < 4 else ([DB("mixs", sname, ti)] if kc < 10
                                                                      else [DB("mixa", sname, ti, kc - 10)])
                        R.dma("sync", lambda e, kc=kc: e.dma_start(out=aTk[kc].ap, in_=mixT[kc, :, t0:t0 + T]),
                              aTk[kc].buf, rd, [aTk[kc].buf], ())
                else:
                    LD(hT, hT.ap, hTd[:, :, t0:t0 + T].rearrange("c p s -> p c s"), reads=[DB("hT", sname, ti)])
                    LD(pf, pf.ap, p_in[sname][l, ti * T:(ti + 1) * T, :].rearrange("(s p) k -> p s k", p=128))

            def fm_block_kc_outer(w, n, evac):
                pss = [nb() for _ in range(n)]
                for kc in range(NDC):
                    for c in range(n):
                        MM(pss[c].ap, w[:, kc, c * 128:(c + 1) * 128], aTk[kc].ap, kc == 0, kc == NDC - 1, [w, aTk[kc]], pss[c],
                           kc == NDC - 1)
                for c in range(n):
                    evac(c, pss[c])

            def fm_chunk(w, c, ps):
                for kc in range(NDC):
                    MM(ps.ap, w[:, kc, c * 128:(c + 1) * 128], aTk[kc].ap, kc == 0, kc == NDC - 1, [w, aTk[kc]], ps,
                       kc == NDC - 1)

            load_tile_inputs(tiles[0], "mix")
            load_tile_inputs(tiles[0], "h")
            for it, (sname, S, base, ti) in enumerate(tiles):
                t0 = base + ti * T
                for bi in range(4):
                    w, _, _ = ring.get()
                    for c in range(4):
                        dc = bi * 4 + c
                        ps = nb()
                        fm_chunk(w, c, ps)
                        pump()
                        V(lambda e, ps=ps, dc=dc: e.tensor_tensor(out=hT[:, dc, :], in0=hT[:, dc, :], in1=ps.ap, op=ALU.add),
                          reads=[ps, hT], adds=[hT])
                        feed(hT[:, dc, :], hT)
                finish(gcol[(l, "mlp")], hT, lambda dc: (aTk[dc].ap, aTk[dc]))

                def evac_up(fc, ps):
                    A(lambda e, ps=ps: e.activation(out=rl.ap, in_=ps.ap, func=AF.Relu), reads=[ps], writes=[rl])
                    V(lambda e, ps=ps, fc=fc: e.tensor_tensor(out=hid[:, fc, :], in0=ps.ap, in1=rl.ap, op=ALU.mult),
                      reads=[ps, rl], adds=[hid])

                for bi in range(16):
                    w, _, _ = ring.get()
                    if bi == 0:
                        fm_block_kc_outer(w, 4, lambda c, ps: evac_up(c, ps))
                        continue
                    for c in range(4):
                        ps = nb()
                        fm_chunk(w, c, ps)
                        evac_up(bi * 4 + c, ps)
                for dc in range(NDC):
                    w, _, _ = ring.get()
                    ps = nb()
                    for kc in range(64):
                        MM(ps.ap, w[:, kc, :], hid[:, kc, :], kc == 0, kc == 63, [w, hid], ps, kc == 63)
                    pump()
                    V(lambda e, ps=ps, dc=dc: e.tensor_tensor(out=hT[:, dc, :], in0=hT[:, dc, :], in1=ps.ap, op=ALU.add),
                      reads=[ps, hT], adds=[hT])
                    feed(hT[:, dc, :], hT)
                V(lambda e: e.tensor_copy(out=pb.ap, in_=pf.ap), reads=[pf], writes=[pb])
                for kc in range(2):
                    for sub in range(4):
                        TR(ps_ptb[:, kc * T + sub * 128:kc * T + (sub + 1) * 128], pb[:, sub, kc * 128:(kc + 1) * 128],
                           ident_b.ap, [pb, ident_b], ps_pt, kc == 1 and sub == 3)
                A(lambda e: e.activation(out=pTt.ap, in_=ps_ptb.ap.rearrange("p (k s) -> p k s", k=2), func=AF.Copy),
                  reads=[ps_pt], writes=[pTt])
                finish(gcol[(l, "ple")], hT, lambda dc: (aTk[dc].ap, aTk[dc]))

                def evac_gate(dc, pg):
                    pp = nb()
                    for kc in range(2):
                        MM(pp.ap, wple[:, kc, dc * 128:(dc + 1) * 128], pTt[:, kc, :], kc == 0, kc == 1, [wple, pTt], pp,
                           kc == 1)
                    if last:
                        pump()
                    A(lambda e, pg=pg: e.activation(out=rl.ap, in_=pg.ap, func=AF.Sigmoid), reads=[pg], writes=[rl])
                    V(lambda e, pp=pp: e.tensor_tensor(out=tmp.ap, in0=pp.ap, in1=rl.ap, op=ALU.mult), reads=[pp, rl],
                      writes=[tmp])
                    V(lambda e, dc=dc: e.tensor_tensor(out=hT[:, dc, :], in0=hT[:, dc, :], in1=tmp.ap, op=ALU.add),
                      reads=[tmp, hT], adds=[hT])
                    if last:
                        feed(hT[:, dc, :], hT)

                for bi in range(4):
                    w, _, _ = ring.get()
                    if bi == 0:
                        fm_block_kc_outer(w, 4, lambda c, pg: evac_gate(c, pg))
                        continue
                    for c in range(4):
                        pg = nb()
                        fm_chunk(w, c, pg)
                        evac_gate(bi * 4 + c, pg)
                if it + 1 < len(tiles):
                    load_tile_inputs(tiles[it + 1], "mix")
                if not last:
                    ST(hT, hTd[:, :, t0:t0 + T].rearrange("c p s -> p c s"), hT.ap, DB("hT", sname, ti), mode="w")
                else:
                    finish(gcol["final"], hT, lambda dc: (hT[:, dc, :], hT))
                    for sub in range(4):
                        for d4 in range(4):
                            ps = nb()
                            for c in range(4):
                                dc = d4 * 4 + c
                                TR(ps[:, c * 128:(c + 1) * 128], hT[:, dc, sub * 128:(sub + 1) * 128], ident_f.ap,
                                   [hT, ident_f], ps, c == 3)
                            copy_out(ystg[:, sub, d4 * 512:(d4 + 1) * 512], ps.ap, [ps], adds=[ystg])
                    ST(ystg, y_out[sname][ti * T:(ti + 1) * T, :].rearrange("(s p) d -> p s d", p=128), ystg.ap,
                       DB("y", sname, ti), mode="w")
                if it + 1 < len(tiles):
                    load_tile_inputs(tiles[it + 1], "h")
            R.barrier()

        stop = False
        for l in range(nlayers):
            for nm, fn in (("p1", phase_p1), ("conv", phase_conv), ("ssd", phase_ssd), ("attn", phase_attn),
                           ("p3", phase_p3)):
                fn(l)
                if stop_after == (l, nm):
                    stop = True
                    break
            if stop:
                break

        finals = [(s, v) for (s, v) in R.pool]
        R.replay(finals)
    return nc


_CACHE = {}


def kernel(**inputs):
    SA, SB = 4096, 2048
    NCORES = 8
    key = (SA, SB)
    if key not in _CACHE:
        _CACHE[key] = build(SA, SB)
    nc = _CACHE[key]
    f32 = lambda a: np.ascontiguousarray(np.asarray(a, dtype=np.float32))
    xp = f32(inputs["x_prompt"])
    xs = f32(inputs["x_sample"])
    pp = f32(inputs["p_prompt"])
    psm = f32(inputs["p_sample"])
    shared = {
        "w_in": f32(inputs["w_in"]), "w_out": f32(inputs["w_out"]), "w_up": f32(inputs["w_up"]),
        "w_down": f32(inputs["w_down"]), "w_ple": f32(inputs["w_ple"]), "w_ple_gate": f32(inputs["w_ple_gate"]),
        "norm_mix_g": f32(inputs["norm_mix_g"]), "norm_mlp_g": f32(inputs["norm_mlp_g"]),
        "norm_ple_g": f32(inputs["norm_ple_g"]), "final_norm_g": f32(inputs["final_norm_g"]),
        "conv_w": f32(inputs["conv_w"]), "conv_b": f32(inputs["conv_b"]),
        "conv_norm_g": f32(inputs["conv_norm_g"]), "conv_norm_b": f32(inputs["conv_norm_b"]),
        "ssd_conv_w": f32(inputs["ssd_conv_w"]), "ssd_conv_b": f32(inputs["ssd_conv_b"]),
        "ssd_dt_bias": f32(inputs["ssd_dt_bias"]).reshape(DEPTH, 24),
        "ssd_a_log": f32(inputs["ssd_a_log"]).reshape(DEPTH, 24),
        "ssd_d": f32(inputs["ssd_d"]), "ssd_norm_g": f32(inputs["ssd_norm_g"]),
        "lambda_q1": f32(inputs["lambda_q1"]), "lambda_k1": f32(inputs["lambda_k1"]),
        "lambda_q2": f32(inputs["lambda_q2"]), "lambda_k2": f32(inputs["lambda_k2"]),
        "attn_subln_g": f32(inputs["attn_subln_g"]), "rel_bias": f32(inputs["rel_bias"]),
        "onehot": onehot_bias_table(),
    }
    in_maps = []
    for c in range(NCORES):
        m = dict(shared)
        m["xa"] = xs[c]
        m["pa"] = np.ascontiguousarray(psm[:, c])
        m["xb"] = xp[c % 4]
        m["pb"] = np.ascontiguousarray(pp[:, c % 4])
        in_maps.append(m)
    res = run_bass_kernel_spmd(nc, in_maps, core_ids=list(range(NCORES)))
    y_sample = np.stack([np.asarray(res.results[c]["ya"], dtype=np.float32) for c in range(NCORES)], axis=0)
    y_prompt = np.stack([np.asarray(res.results[c]["yb"], dtype=np.float32) for c in range(4)], axis=0)
    return (y_prompt, y_sample)
```
